# Optimizing a Trainium2 kernel written in Bass

```python
import jax, jax.numpy as jnp
from jax import lax
import numpy as np

D_MODEL = 1024
BATCH = 4
SEQ = 4096
DEPTH = 1

CHUNK = 64
SSM_INNER = 1024
SSM_HEAD_DIM = 64
SSM_HEADS = SSM_INNER // SSM_HEAD_DIM
SSM_GROUPS = 2
SSM_STATE = 128
SSM_CONV = 4
SSM_CONV_DIM = SSM_INNER + 2 * SSM_GROUPS * SSM_STATE
LRU_WIDTH = 1024
LRU_BLOCKS = 16
LRU_BLOCK = LRU_WIDTH // LRU_BLOCKS
LRU_CONV = 4
LRU_C = 8.0
FFN_DIM = 3072
FFN_CONV = 3
N_BRANCHES = 2
RMS_EPS = 1e-6
IN_SPLITS = (
    SSM_INNER,
    SSM_INNER + SSM_CONV_DIM,
    SSM_INNER + SSM_CONV_DIM + SSM_HEADS,
    SSM_INNER + SSM_CONV_DIM + SSM_HEADS + LRU_WIDTH,
    SSM_INNER + SSM_CONV_DIM + SSM_HEADS + 2 * LRU_WIDTH,
)
N_IN = SSM_INNER + SSM_CONV_DIM + SSM_HEADS + 2 * LRU_WIDTH + N_BRANCHES * D_MODEL

kernel_name = "hybrid_ssd_rglru_gated_merge_convffn"


def rmsnorm(x, w):
    xf = x.astype(jnp.float32)
    y = xf * lax.rsqrt(jnp.mean(xf * xf, axis=-1, keepdims=True) + RMS_EPS)
    return (y * w.astype(jnp.float32)).astype(x.dtype)


def causal_dwconv(x, w, b):
    k = w.shape[0]
    out = lax.conv_general_dilated(
        x, w[:, None, :].astype(x.dtype), window_strides=(1,), padding=[(k - 1, 0)],
        dimension_numbers=("NWC", "WIO", "NWC"), feature_group_count=x.shape[-1])
    return out + b


def ssd_chunked(xh, dt, a, bm, cm):
    b, s = xh.shape[:2]
    c = s // CHUNK
    e = SSM_HEADS // SSM_GROUPS
    x = xh.reshape(b, c, CHUNK, SSM_GROUPS, e, SSM_HEAD_DIM)
    dtc = dt.reshape(b, c, CHUNK, SSM_GROUPS, e)
    bc = bm.reshape(b, c, CHUNK, SSM_GROUPS, SSM_STATE)
    cc = cm.reshape(b, c, CHUNK, SSM_GROUPS, SSM_STATE)
    a_dt = (dtc * a.reshape(SSM_GROUPS, e)).transpose(0, 3, 4, 1, 2)
    a_cs = jnp.cumsum(a_dt, axis=-1)
    xdt = x * dtc[..., None]
    causal = jnp.tril(jnp.ones((CHUNK, CHUNK), dtype=bool))
    seg = a_cs[..., :, None] - a_cs[..., None, :]
    decay = jnp.exp(jnp.where(causal, seg, -jnp.inf))
    cb = jnp.einsum("bclgn,bcsgn->bgcls", cc, bc)
    y_diag = jnp.einsum("bgcls,bgecls,bcsgep->bclgep", cb, decay, xdt)
    decay_states = jnp.exp(a_cs[..., -1:] - a_cs)
    states = jnp.einsum("bclgn,bgecl,bclgep->cbgepn", bc, decay_states, xdt)
    chunk_decay = jnp.exp(a_cs[..., -1]).transpose(3, 0, 1, 2)

    def step(h, inp):
        st, dec = inp
        return h * dec[..., None, None] + st, h

    h0 = jnp.zeros(states.shape[1:], states.dtype)
    _, prev = lax.scan(step, h0, (states, chunk_decay))
    y_off = jnp.einsum("bclgn,cbgepn,bgecl->bclgep", cc, prev, jnp.exp(a_cs))
    return (y_diag + y_off).reshape(b, s, SSM_HEADS, SSM_HEAD_DIM)


def ssd_mixer(u_z, u_xbc, u_dt, conv_w, conv_b, dt_bias, a_log, d_skip, norm_w):
    b, s = u_z.shape[:2]
    xbc = jax.nn.silu(causal_dwconv(u_xbc, conv_w, conv_b)).astype(jnp.float32)
    xs, bm, cm = jnp.split(xbc, [SSM_INNER, SSM_INNER + SSM_GROUPS * SSM_STATE], axis=-1)
    dt = jax.nn.softplus(u_dt.astype(jnp.float32) + dt_bias.astype(jnp.float32))
    a = -jnp.exp(a_log.astype(jnp.float32))
    xh = xs.reshape(b, s, SSM_HEADS, SSM_HEAD_DIM)
    y = ssd_chunked(xh, dt, a,
                    bm.reshape(b, s, SSM_GROUPS, SSM_STATE),
                    cm.reshape(b, s, SSM_GROUPS, SSM_STATE))
    y = y + d_skip.astype(jnp.float32)[:, None] * xh
    yg = (y.reshape(b, s, SSM_INNER) * jax.nn.silu(u_z.astype(jnp.float32)))
    yg = yg.reshape(b, s, SSM_GROUPS, SSM_INNER // SSM_GROUPS)
    yg = yg * lax.rsqrt(jnp.mean(yg * yg, axis=-1, keepdims=True) + RMS_EPS)
    yg = yg.reshape(b, s, SSM_INNER) * norm_w.astype(jnp.float32)
    return yg.astype(u_z.dtype)


def rglru_mixer(u_y, u_x, conv_w, conv_b, wr, br, wi, bi, lam):
    b, s = u_x.shape[:2]
    gate = jax.nn.gelu(u_y, approximate=True)
    xc = causal_dwconv(u_x, conv_w, conv_b)
    xb = xc.reshape(b, s, LRU_BLOCKS, LRU_BLOCK)
    r = jax.nn.sigmoid(jnp.einsum("bshi,hij->bshj", xb, wr) + br).reshape(b, s, LRU_WIDTH)
    i = jax.nn.sigmoid(jnp.einsum("bshi,hij->bshj", xb, wi) + bi).reshape(b, s, LRU_WIDTH)
    log_a = -LRU_C * r.astype(jnp.float32) * jax.nn.softplus(-lam.astype(jnp.float32))
    a = jnp.exp(log_a)
    mult = jnp.sqrt(-jnp.expm1(2.0 * log_a))
    bx = mult * i.astype(jnp.float32) * xc.astype(jnp.float32)

    def combine(lhs, rhs):
        a1, b1 = lhs
        a2, b2 = rhs
        return a1 * a2, a2 * b1 + b2

    _, h = lax.associative_scan(combine, (a, bx), axis=1)
    return (h.astype(u_x.dtype) * gate)


def setup_inputs(seed: int = 0) -> dict:
    key = jax.random.key(seed)
    ks = jax.random.split(key, 32)
    L = DEPTH
    f32 = jnp.float32

    def nrm(k, shape, scale):
        return jax.random.normal(k, shape, f32) * scale

    def gain(k, shape):
        return 1.0 + 0.05 * jax.random.normal(k, shape, f32)

    dt0 = jnp.exp(jax.random.uniform(ks[8], (L, SSM_HEADS), f32, np.log(1e-3), np.log(1e-1)))
    a_c = jax.random.uniform(ks[16], (L, LRU_WIDTH), f32, 0.9, 0.999)
    a0 = a_c ** (1.0 / LRU_C)
    return {
        "x": nrm(ks[0], (BATCH, SEQ, D_MODEL), 1.0),
        "mix_pre_norm": gain(ks[1], (L, D_MODEL)),
        "mix_post_norm": gain(ks[2], (L, D_MODEL)),
        "w_in": nrm(ks[3], (L, D_MODEL, N_IN), D_MODEL ** -0.5),
        "ssm_conv_w": nrm(ks[4], (L, SSM_CONV, SSM_CONV_DIM), SSM_CONV ** -0.5),
        "ssm_conv_b": nrm(ks[5], (L, SSM_CONV_DIM), 0.02),
        "ssm_dt_bias": dt0 + jnp.log(-jnp.expm1(-dt0)),
        "ssm_a_log": jnp.log(jax.random.uniform(ks[6], (L, SSM_HEADS), f32, 1.0, 16.0)),
        "ssm_d": 1.0 + 0.1 * jax.random.normal(ks[7], (L, SSM_HEADS), f32),
        "ssm_norm": gain(ks[9], (L, SSM_INNER)),
        "w_proj_ssm": nrm(ks[10], (L, SSM_INNER, D_MODEL), SSM_INNER ** -0.5),
        "lru_conv_w": nrm(ks[11], (L, LRU_CONV, LRU_WIDTH), LRU_CONV ** -0.5),
        "lru_conv_b": nrm(ks[12], (L, LRU_WIDTH), 0.02),
        "lru_wr": nrm(ks[13], (L, LRU_BLOCKS, LRU_BLOCK, LRU_BLOCK), LRU_BLOCK ** -0.5),
        "lru_br": nrm(ks[14], (L, LRU_BLOCKS, LRU_BLOCK), 0.02),
        "lru_wi": nrm(ks[15], (L, LRU_BLOCKS, LRU_BLOCK, LRU_BLOCK), LRU_BLOCK ** -0.5),
        "lru_bi": nrm(ks[17], (L, LRU_BLOCKS, LRU_BLOCK), 0.02),
        "lru_lambda": jnp.log(a0) - jnp.log1p(-a0),
        "w_proj_lru": nrm(ks[18], (L, LRU_WIDTH, D_MODEL), LRU_WIDTH ** -0.5),
        "gate_b": nrm(ks[19], (L, N_BRANCHES, D_MODEL), 0.02),
        "w_out": nrm(ks[20], (L, D_MODEL, D_MODEL), D_MODEL ** -0.5),
        "ffn_pre_norm": gain(ks[21], (L, D_MODEL)),
        "ffn_post_norm": gain(ks[22], (L, D_MODEL)),
        "w_ffn_up": nrm(ks[23], (L, D_MODEL, 2 * FFN_DIM), D_MODEL ** -0.5),
        "ffn_conv_w": nrm(ks[24], (L, FFN_CONV, 2 * FFN_DIM), FFN_CONV ** -0.5),
        "ffn_conv_b": nrm(ks[25], (L, 2 * FFN_DIM), 0.02),
        "w_ffn_down": nrm(ks[26], (L, FFN_DIM, D_MODEL), FFN_DIM ** -0.5),
    }


def reference(x, mix_pre_norm, mix_post_norm, w_in, ssm_conv_w, ssm_conv_b, ssm_dt_bias,
              ssm_a_log, ssm_d, ssm_norm, w_proj_ssm, lru_conv_w, lru_conv_b, lru_wr, lru_br,
              lru_wi, lru_bi, lru_lambda, w_proj_lru, gate_b, w_out, ffn_pre_norm,
              ffn_post_norm, w_ffn_up, ffn_conv_w, ffn_conv_b, w_ffn_down):
    b, s, _ = x.shape
    for l in range(DEPTH):
        h = rmsnorm(x, mix_pre_norm[l])
        u = h @ w_in[l]
        u_z, u_xbc, u_dt, u_ly, u_lx, u_g = jnp.split(u, list(IN_SPLITS), axis=-1)
        y_a = ssd_mixer(u_z, u_xbc, u_dt, ssm_conv_w[l], ssm_conv_b[l], ssm_dt_bias[l],
                        ssm_a_log[l], ssm_d[l], ssm_norm[l])
        y_b = rglru_mixer(u_ly, u_lx, lru_conv_w[l], lru_conv_b[l], lru_wr[l], lru_br[l],
                          lru_wi[l], lru_bi[l], lru_lambda[l])
        g = jax.nn.sigmoid(u_g.reshape(b, s, N_BRANCHES, D_MODEL) + gate_b[l])
        merged = g[:, :, 0, :] * (y_a @ w_proj_ssm[l]) + g[:, :, 1, :] * (y_b @ w_proj_lru[l])
        x = x + rmsnorm(merged @ w_out[l], mix_post_norm[l])
        h = rmsnorm(x, ffn_pre_norm[l])
        up = causal_dwconv(h @ w_ffn_up[l], ffn_conv_w[l], ffn_conv_b[l])
        f_gate, f_val = jnp.split(up, 2, axis=-1)
        f = (jax.nn.gelu(f_gate, approximate=True) * f_val) @ w_ffn_down[l]
        x = x + rmsnorm(f, ffn_post_norm[l])
    return x
```

```python
import contextlib
import numpy as np
import ml_dtypes
import concourse.bass as bass
import concourse.mybir as mybir
from concourse.bass_utils import run_bass_kernel_spmd

F32 = mybir.dt.float32
BF16 = mybir.dt.bfloat16
AF = mybir.ActivationFunctionType
ALU = mybir.AluOpType
AX = mybir.AxisListType

ENGS = ("sync", "scalar", "vector", "gpsimd", "tensor")


class Buf:
    __slots__ = ("name", "w", "r", "tw", "tr")

    def __init__(self, name):
        self.name = name
        self.w = None
        self.r = []
        self.tw = 0.0
        self.tr = 0.0


class Tile:
    def __init__(self, t, name):
        self.t = t
        self.buf = Buf(name)

    def __getitem__(self, k):
        return self.t[k]


class Prog:
    def __init__(self, nc):
        self.nc = nc
        self.es = contextlib.ExitStack()
        self.q = {e: [] for e in ENGS}
        self.sem = {}
        self.cnt = {}
        self.seen = {e: {} for e in ENGS}
        for e in ENGS:
            self.newsem("E_" + e)
        self.n_t = 0
        self.fifo = None
        self.act_set = None
        self.eng_free = {e: 0.0 for e in ENGS}

    def newsem(self, name):
        if name not in self.sem:
            self.sem[name] = self.es.enter_context(self.nc.semaphore(name))
            self.cnt[name] = 0
        return name

    def sb(self, shape, dtype, name=None):
        self.n_t += 1
        name = "s_" + (name or f"t{self.n_t}")
        t = self.es.enter_context(self.nc.sbuf_tensor(name, list(shape), dtype))
        return Tile(t, name)

    def ps(self, shape, dtype, name=None):
        self.n_t += 1
        name = "ps_" + (name or f"p{self.n_t}")
        t = self.es.enter_context(self.nc.psum_tensor(name, list(shape), dtype))
        return Tile(t, name)

    def _deps(self, eng, R, W):
        need = {}
        for b in R:
            b = b.buf if isinstance(b, Tile) else b
            if b.w is not None:
                s, v = b.w
                need[s] = max(need.get(s, 0), v)
        for b in W:
            b = b.buf if isinstance(b, Tile) else b
            if b.w is not None:
                s, v = b.w
                need[s] = max(need.get(s, 0), v)
            for s, v in b.r:
                need[s] = max(need.get(s, 0), v)
        waits = []
        seen = self.seen[eng]
        for s, v in need.items():
            if eng == "tensor" and s == "E_tensor":
                continue
            if seen.get(s, 0) < v:
                seen[s] = v
                waits.append((s, v))
        return waits

    def _mark(self, tok, R, W):
        for b in R:
            b = b.buf if isinstance(b, Tile) else b
            b.r.append(tok)
            if len(b.r) > 24:
                m = {}
                for s, v in b.r:
                    m[s] = max(m.get(s, 0), v)
                b.r = list(m.items())
        for b in W:
            b = b.buf if isinstance(b, Tile) else b
            b.w = tok
            b.r = []

    @staticmethod
    def _bufs(xs):
        return tuple(b.buf if isinstance(b, Tile) else b for b in xs)

    def op(self, eng, fn, R=(), W=(), signal=True, dur=0.5, aset=None):
        d = ("op", eng, fn, self._bufs(R), self._bufs(W), signal, dur, aset)
        if self.fifo is not None:
            self.fifo.append(d)
        else:
            self._emit(d)

    def dma(self, queue, out, in_, sem, R=(), W=(), dur=8.0):
        self.newsem(sem)
        d = ("dma", queue, (out, in_, sem), self._bufs(R), self._bufs(W), True, dur, None)
        if self.fifo is not None:
            self.fifo.append(d)
            return (sem, self.cnt[sem] + 16)
        return self._emit(d)

    def defer(self, cb):
        if self.fifo is not None:
            self.fifo.append(("cb", cb))
        else:
            cb()

    def est_start(self, d):
        _, eng, _, R, W, _, _, aset = d
        t = self.eng_free[eng]
        for b in R:
            t = max(t, b.tw)
        for b in W:
            t = max(t, b.tw, b.tr)
        if aset is not None and aset != self.act_set:
            t += 2.5
        return t

    def _emit(self, d):
        kind, eng, fn, R, W, signal, dur, aset = d
        start = self.est_start(d)
        if aset is not None:
            self.act_set = aset
        waits = self._deps(eng, R, W)
        if kind == "dma":
            out, in_, sem = fn
            self.cnt[sem] += 16
            tok = (sem, self.cnt[sem])
            self.q[eng].append((waits, lambda e: e.dma_start(out=out, in_=in_), (sem, 16)))
            self.eng_free[eng] = start + 1.0
            fin = start + dur
        else:
            sname = "E_" + eng
            if signal:
                self.cnt[sname] += 1
                tok = (sname, self.cnt[sname])
            else:
                tok = (sname, self.cnt[sname] + 1)
            self.q[eng].append((waits, fn, (sname, 1) if signal else None))
            fin = start + dur
            self.eng_free[eng] = fin
        fin_vis = fin + 0.15
        for b in R:
            b.tr = max(b.tr, fin_vis)
        for b in W:
            b.tw = fin_vis
            b.tr = 0.0
        self._mark(tok, R, W)
        return tok

    def mm(self, out, pairs, R=(), W=(), signal=True):
        n = len(pairs)

        def fn(e):
            inst = None
            for i, (l, r) in enumerate(pairs):
                inst = e.matmul(out, l, r, start=(i == 0), stop=(i == n - 1))
            return inst
        try:
            nfree = pairs[0][1].free_size()
        except Exception:
            nfree = 512
        return self.op("tensor", fn, R, W, signal, dur=n * (0.07 + nfree / 1900.0))

    def wait_all(self, eng, toks):
        waits = []
        for s, v in toks:
            if self.seen[eng].get(s, 0) < v:
                self.seen[eng][s] = v
                waits.append((s, v))
        self.q[eng].append((waits, None, None))

    def emit(self):
        nc = self.nc
        with nc.Block() as block:
            def make(eng_name):
                def body(eng):
                    for waits, fn, inc in self.q[eng_name]:
                        for s, v in waits:
                            eng.wait_ge(self.sem[s], v)
                        if fn is not None:
                            inst = fn(eng)
                            if inc is not None:
                                inst.then_inc(self.sem[inc[0]], inc[1])
                return body
            for e in ENGS:
                if self.q[e]:
                    getattr(block, e)(make(e))
        self.es.close()


D = 1024
NIN = 6672
FF = 3072
T = 512
NJ = T // 128
EPS = 1e-6

O_CWS, O_CBS, O_CWL, O_CBL, O_BR, O_BI, O_LAM, O_GB, O_CWF, O_CBF, NCV = 0, 48, 60, 92, 100, 108, 116, 124, 140, 284, 332
R_PN1, R_PN2, R_SN, R_W1, R_W2, R_DTB, R_ALOG, R_D, NRB = 0, 1024, 2048, 3072, 4096, 5120, 5136, 5152, 5168

def _groups(mode="F"):
    g = []
    if mode != "S":
        for h in range(2):
            g.append(("w_in", 0, h * 512))
    for h in range(3):
        g.append(("w_in", 0, 1024 + h * 512))
    if mode != "S":
        for h in range(2):
            g.append(("w_in", 0, 2576 + h * 512))
    for h in range(2):
        g.append(("w_in", 0, 3600 + h * 512))
    if mode == "S":
        return g
    for q in range(4):
        g.append([("p_a", 0, q * 256, 256, 0), ("p_b", 0, q * 256, 256, 256)])
        g.append([("w_in", 0, 4624 + q * 256, 256, 0), ("w_in", 0, 4624 + 1024 + q * 256, 256, 256)])
    for h in range(2):
        g.append(("w_out", 0, h * 512))
    for q in range(6):
        g.append(("w_up", 0, q * 512))
        g.append(("w_up", 0, 3072 + q * 512))
    if mode == "H":
        return g
    for h in range(2):
        for kb in range(3):
            g.append(("w_down", kb, h * 512))
    return g


NSLOT = 3


def build_program(modes, dbg=False):
    nc = bass.Bass("TRN2", target_bir_lowering=False)
    if isinstance(modes, int):
        modes = ["F"] * modes
    NT = len(modes)
    S = NT * T
    NF = sum(1 for m in modes if m == "F")
    ALLG = _groups("F")
    key = lambda g: repr(g)
    gid_of = {key(g): i for i, g in enumerate(ALLG)}
    GROUPS = []
    for m in modes:
        GROUPS += [gid_of[key(g)] for g in _groups(m)]
    dram = {}
    dram["x"] = nc.dram_tensor("x", [S, D], F32, kind="ExternalInput").ap()
    dram["w_in"] = nc.dram_tensor("w_in", [D, NIN], F32, kind="ExternalInput").ap()
    dram["p_a"] = nc.dram_tensor("p_a", [D, D], F32, kind="ExternalInput").ap()
    dram["p_b"] = nc.dram_tensor("p_b", [D, D], F32, kind="ExternalInput").ap()
    dram["w_out"] = nc.dram_tensor("w_out", [D, D], F32, kind="ExternalInput").ap()
    dram["w_up"] = nc.dram_tensor("w_up", [D, 2 * FF], F32, kind="ExternalInput").ap()
    dram["w_down"] = nc.dram_tensor("w_down", [FF, D], F32, kind="ExternalInput").ap()
    d_bd = nc.dram_tensor("bd", [128, 16, 128], F32, kind="ExternalInput").ap()
    d_cvec = nc.dram_tensor("cvec", [128, NCV], F32, kind="ExternalInput").ap()
    d_rowb = nc.dram_tensor("rowb", [128, NRB], F32, kind="ExternalInput").ap()
    d_ident = nc.dram_tensor("ident", [128, 128], BF16, kind="ExternalInput").ap()
    d_tri = nc.dram_tensor("tri", [128, 3, 128], F32, kind="ExternalInput").ap()
    d_out = nc.dram_tensor("out", [NF * T, D], F32, kind="ExternalOutput").ap()
    d_flag = nc.dram_tensor("flag", [128, 1], F32, kind="ExternalInput").ap()

    P = Prog(nc)
    op = P.op
    dbg_toks = []

    def dump(name, ap, shape, dtype, R):
        if not dbg:
            return
        d = nc.dram_tensor("dbg_" + name, list(shape), dtype, kind="ExternalOutput").ap()
        dbg_toks.append(P.dma("sync", d, ap, "d_dbg_" + name, R=R))

    def vdur(out, eng="vector"):
        n = out.free_size()
        return (0.12 + n / 960.0) if eng == "vector" else (0.25 + n / 450.0)

    ASET = {AF.Exp: "ln_exp", AF.Ln: "ln_exp", AF.Sigmoid: "sig", AF.Silu: "silu", AF.Gelu_apprx_tanh: "gelu"}

    def ACT(out, in_, func, R, W, **kw):
        return op("scalar", lambda e: e.activation(out=out, in_=in_, func=func, **kw), R, W, dur=0.22 + out.free_size() / 1200.0,
                  aset=ASET.get(func))

    def TT(out, a, b, alu, R, W, eng="vector"):
        return op(eng, lambda e: e.tensor_tensor(out=out, in0=a, in1=b, op=alu), R, W, dur=vdur(out, eng))

    def STT(out, in0, scalar, in1, op0, op1, R, W):
        return op("vector", lambda e: e.scalar_tensor_tensor(out=out, in0=in0, scalar=scalar, in1=in1, op0=op0, op1=op1), R, W,
                  dur=vdur(out))

    def TS(out, in0, s1, s2, op0, op1, R, W, eng="vector"):
        if s2 is None:
            return op(eng, lambda e: e.tensor_scalar(out=out, in0=in0, scalar1=s1, scalar2=None, op0=op0), R, W, dur=vdur(out, eng))
        return op(eng, lambda e: e.tensor_scalar(out=out, in0=in0, scalar1=s1, scalar2=s2, op0=op0, op1=op1), R, W, dur=vdur(out, eng))

    def CP(out, in_, R, W, eng="vector"):
        return op(eng, lambda e: e.tensor_copy(out=out, in_=in_), R, W, dur=vdur(out, eng))

    def bc(ap, shape):
        return ap.unsqueeze(2).broadcast_to(shape)

    cvec = P.sb([128, NCV], F32, "cvec")
    rowb = P.sb([128, NRB], F32, "rowb")
    ident = P.sb([128, 128], BF16, "ident")
    tri = P.sb([128, 3, 128], F32, "tri")
    bdf = P.sb([128, 16, 128], F32, "bdf")
    bd = P.sb([128, 16, 128], BF16, "bd")
    wdt = P.sb([128, 8, 16], BF16, "wdt")
    flag = P.sb([128, 1], F32, "flag")
    consts = [cvec, rowb, ident, tri, bdf, flag]
    P.dma("sync", flag[:], d_flag, "d_const", W=[flag])
    P.dma("sync", cvec[:], d_cvec, "d_const", W=[cvec])
    P.dma("sync", rowb[:], d_rowb, "d_const", W=[rowb])
    P.dma("sync", ident[:], d_ident, "d_const", W=[ident])
    P.dma("sync", tri[:], d_tri, "d_const", W=[tri])
    P.dma("sync", bdf[:], d_bd, "d_const", W=[bdf])
    for c in consts:
        c.buf.w = ("d_const", P.cnt["d_const"])
    P.dma("gpsimd", wdt[:], dram["w_in"].rearrange("(kc p) c -> p kc c", p=128)[:, :, 2560:2576], "d_wdt", W=[wdt])
    CP(bd[:], bdf[:], [bdf], [bd])
    U = tri[:, 0, :]
    G = tri[:, 1, :]
    ONES = tri[:, 2, :]
    cst = P.sb([128, 64], F32, "cst")
    ACT(cst[:, 0:16], rowb[:, R_ALOG:R_ALOG + 16], AF.Exp, [rowb], [cst])
    TS(cst[:, 0:16], cst[:, 0:16], -1.0, None, ALU.mult, None, [cst], [cst])
    ACT(cst[:, 32:40], cvec[:, O_LAM:O_LAM + 8], AF.Exp, [cvec], [cst], scale=-1.0)
    ACT(cst[:, 32:40], cst[:, 32:40], AF.Ln, [cst], [cst], bias=1.0)
    TS(cst[:, 16:24], cst[:, 32:40], -8.0, None, ALU.mult, None, [cst], [cst])
    TS(cst[:, 24:32], cst[:, 32:40], -16.0, None, ALU.mult, None, [cst], [cst])
    AROW = cst[:, 0:16]

    ssm_state = P.sb([128, 1024], F32, "ssm_state")
    ssm_state_bf = P.sb([128, 1024], BF16, "ssm_state_bf")
    lru_h = P.sb([128, 8], F32, "lru_h")
    halo_s = P.sb([128, 12, 3], F32, "halo_s")
    halo_l = P.sb([128, 8, 3], F32, "halo_l")
    halo_f = P.sb([128, 48, 2], F32, "halo_f")
    op("gpsimd", lambda e: e.memset(ssm_state[:], 0.0), [], [ssm_state])
    op("gpsimd", lambda e: e.memset(ssm_state_bf[:], 0.0), [], [ssm_state_bf])
    op("gpsimd", lambda e: e.memset(lru_h[:], 0.0), [], [lru_h])
    op("gpsimd", lambda e: e.memset(halo_s[:], 0.0), [], [halo_s])
    op("gpsimd", lambda e: e.memset(halo_l[:], 0.0), [], [halo_l])
    op("gpsimd", lambda e: e.memset(halo_f[:], 0.0), [], [halo_f])

    slots = [P.sb([128, 8, 512], BF16, f"slot{i}") for i in range(NSLOT)]
    ring = {"issued": 0, "res": 0, "done": 0}
    need = {}
    total_groups = len(GROUPS)

    wscr = nc.dram_tensor("wscr", [len(ALLG), 128, 8 * 512], BF16).ap()
    cvbuf = [Buf(f"cv{i}") for i in range(len(ALLG))]
    first_use = []
    for g in GROUPS:
        if g not in first_use:
            first_use.append(g)
    dgscr = nc.dram_tensor("dgscr", [20, 128, 512], BF16).ap()
    dgbuf = [Buf(f"dg{i}") for i in range(20)]
    conv_todo = list(first_use)

    def issue_conversions(k, only=None):
        for _ in range(min(k, len(conv_todo))):
            if only is not None:
                if only not in conv_todo:
                    return
                conv_todo.remove(only)
                gid = only
            else:
                gid = conv_todo.pop(0)
            parts = ALLG[gid]
            if isinstance(parts, tuple):
                parts = [parts + (512, 0)]
            for name, rb, c0, n, off in parts:
                src = dram[name][rb * 1024:(rb + 1) * 1024, :].rearrange("(kc p) c -> p kc c", p=128)[:, :, c0:c0 + n]
                dst = wscr[gid].rearrange("p (kc c) -> p kc c", c=512)[:, :, off:off + n]
                P.dma("gpsimd", dst, src, f"d_cv{gid}", W=[cvbuf[gid]], dur=12.0)

    n_first = len(_groups(modes[0]))
    issue_conversions(n_first)
    n_S = sum(1 for m in modes if m == "S")
    conv_per_tile = -(-(len(first_use) - n_first) // max(n_S, 1))

    def issue_load():
        gi = ring["issued"]
        gid = GROUPS[gi]
        issue_conversions(1, only=gid)
        sl = slots[gi % NSLOT]
        P.dma("sync", sl[:], wscr[gid].rearrange("p (kc c) -> p kc c", c=512), f"d_slot{gi % NSLOT}", R=[cvbuf[gid]], W=[sl], dur=5.0)
        ring["issued"] += 1

    def pump():
        while ring["done"] < ring["res"] and need.get(ring["done"], 1) == 0:
            ring["done"] += 1
        while ring["issued"] < total_groups and ring["issued"] < ring["done"] + NSLOT:
            issue_load()

    class GH:
        def __init__(self, n=1, readers=1):
            self.n, self.readers, self.first = n, readers, None

        def get(self):
            if self.first is None:
                self.first = ring["res"]
                ring["res"] += self.n
                for g in range(self.first, self.first + self.n):
                    need[g] = self.readers
            k = 0
            while True:
                pump()
                if ring["issued"] >= self.first + self.n:
                    break
                k += 1
                assert k < 200000, "ring deadlock"
                yield "blocked"
            return [slots[(self.first + i) % NSLOT] for i in range(self.n)]

        def release(self):
            def cb():
                for g in range(self.first, self.first + self.n):
                    need[g] -= 1
                pump()
            P.defer(cb)

    class RPool:
        def __init__(self, items):
            self.free = list(items)

        def get(self, n=1):
            k = 0
            while len(self.free) < n:
                k += 1
                assert k < 200000, "resource deadlock"
                yield "blocked"
            out = self.free[:n]
            del self.free[:n]
            return out

        def put(self, xs):
            xs = list(xs)
            P.defer(lambda: self.free.extend(xs))

    def seq(gen):
        try:
            while True:
                next(gen)
        except StopIteration as e:
            return e.value

    class Chain:
        def __init__(self, gen):
            self.gen, self.fifo, self.done = gen, [], False

    def advance(c):
        while not c.fifo and not c.done:
            P.fifo = c.fifo
            try:
                r = next(c.gen)
            except StopIteration:
                c.done = True
                r = None
            P.fifo = None
            if r == "blocked":
                return

    def run(chains, W, extra=(), extra_steps=1):
        act = [Chain(g) for g in extra]
        for c in act:
            c.extra = True
        it = iter(chains)
        pending = True
        nwin = 0
        guard = 0
        while True:
            while pending and sum(1 for c in act if not getattr(c, "extra", False)) < W:
                try:
                    act.append(Chain(next(it)))
                except StopIteration:
                    pending = False
            progressed = False
            for c in act:
                if not c.fifo and not c.done:
                    advance(c)
                while c.fifo and c.fifo[0][0] == "cb":
                    c.fifo.pop(0)[1]()
                    progressed = True
            before = len(act)
            act = [c for c in act if c.fifo or not c.done]
            if len(act) != before:
                progressed = True
            cands = [c for c in act if c.fifo]
            if not cands:
                if not act and not pending:
                    break
                guard += 1
                assert progressed or guard < 100000, "scheduler deadlock"
                continue
            guard = 0
            best = min(cands, key=lambda c: P.est_start(c.fifo[0]))
            P._emit(best.fifo.pop(0))

    ptr = P.ps([128, 1024], BF16, "ptr")
    banks = RPool([P.ps([128, 512], F32, f"bank{i}") for i in range(7)])

    x_tok = [P.sb([128, 1024], F32, f"x_tok{j}") for j in range(NJ)]
    h_tok = [P.sb([128, 1024], BF16, "h_tok0")] * 2
    junk = P.sb([128, 512], BF16, "junk")
    hT = P.sb([128, 8, T], BF16, "hT")
    stat_n = P.sb([128, 8], F32, "stat_n")
    stat_g = [P.sb([128, 4], F32, f"stat_g{j}") for j in range(NJ)]
    stat_o = [P.sb([128, 4], F32, f"stat_o{j}") for j in range(NJ)]
    stat_f = P.sb([128, 16], F32, "stat_f")
    regA = P.sb([128, 24 * T], BF16, "regA")
    siluz = [Tile(regA[:, j * 1024:(j + 1) * 1024], f"siluz{j}") for j in range(NJ)]
    xsT = [Tile(regA[:, 4096 + c * T: 4096 + (c + 1) * T], f"xsT{c}") for c in range(8)]
    mergedT = Tile(regA[:, 4096:8192].rearrange("p (c t) -> p c t", t=T), "mergedT")
    xs_tok = [Tile(regA[:, 8192 + j * 1024: 8192 + (j + 1) * 1024], f"xs_tok{j}") for j in range(NJ)]
    fT = Tile(regA[:, :].rearrange("p (c t) -> p c t", t=T), "fT")
    regB = P.sb([128, 4 * 1024], F32, "regB")
    regB_bf = regB[:, :].bitcast(BF16)
    gy = [Tile(regB_bf[:, c * T:(c + 1) * T], f"gy{c}") for c in range(8)]
    y_bT = Tile(regB_bf[:, 4096:8192].rearrange("p (c t) -> p c t", t=T), "y_bT")
    d_tok = [Tile(regB[:, j * 1024:(j + 1) * 1024], f"d_tok{j}") for j in range(NJ)]
    BT = [P.sb([128, T], BF16, f"BT{g}") for g in range(2)]
    CT = [P.sb([128, T], BF16, f"CT{g}") for g in range(2)]
    B_tok = [P.sb([128, 256], BF16, f"B_tok{j}") for j in range(NJ)]
    y_aT = P.sb([128, 8, T], BF16, "y_aT")
    NGEN = 12
    gens = RPool([P.sb([128, 3 + T], F32, f"gen{i}") for i in range(NGEN)])
    bdf_flat = bdf[:, :, :].rearrange("p a b -> p (a b)")
    ssd_sets = []
    for i in range(2):
        st = {}
        st["dtb"] = P.sb([128, 64], F32, f"dtb{i}")
        st["exb"] = P.sb([128, 48], F32, f"exb{i}")
        st["ue4"] = P.sb([128, 4, 128], F32, f"ue4_{i}")
        st["exps"] = P.sb([128, 4, 128], F32, f"exps{i}")
        st["MT"] = P.sb([128, 16, 128], BF16, f"MT{i}")
        st["cbTm"] = P.sb([128, 2, 128], F32, f"cbTm{i}")
        st["ybuf"] = P.sb([128, 1024], F32, f"ybuf{i}")
        st["xdt"] = P.sb([128, 1024], BF16, f"xdt{i}")
        if i == 0:
            st["xs_dec"] = P.sb([128, 1024], BF16, "xs_dec0")
            st["ya_tok"] = P.sb([128, 1024], BF16, "ya_tok0")
            st["xsD"] = Tile(bdf_flat[:, 0:512].bitcast(BF16), "xsD0")
            st["alias"] = [st["xsD"]]
        else:
            st["xsD"] = Tile(bdf_flat[:, 512:1024].bitcast(BF16), "xsD1")
            st["xs_dec"] = Tile(bdf_flat[:, 1024:1536].bitcast(BF16), "xs_dec1")
            st["ya_tok"] = Tile(bdf_flat[:, 1536:2048].bitcast(BF16), "ya_tok1")
            st["alias"] = [st["xsD"], st["xs_dec"], st["ya_tok"]]
        ssd_sets.append(st)
    ssds = RPool(ssd_sets)
    state_ver = {}
    eps_t = P.sb([128, 1], F32, "eps_t")
    op("gpsimd", lambda e: e.memset(eps_t[:], EPS), [], [eps_t])

    def retire(olds, news):
        m = {}
        for o in olds:
            toks = list(o.buf.r)
            if o.buf.w is not None:
                toks.append(o.buf.w)
            for s, v in toks:
                m[s] = max(m.get(s, 0), v)
        for n in news:
            n.buf.w = None
            n.buf.r = list(m.items())

    def rstd_from_ss(out_ap, ss_ap, n, tl):
        ACT(out_ap, ss_ap, AF.Ln, [tl, eps_t], [tl], scale=1.0 / n, bias=eps_t[:, 0:1])
        ACT(out_ap, out_ap, AF.Exp, [tl], [tl], scale=-0.5)

    def norm_and_transpose(w_off, jl=range(NJ)):
        if len(jl) < NJ:
            op("gpsimd", lambda e: e.memset(stat_n[:, 0:4], 1.0), [], [stat_n])
        for j in jl:
            ACT(h_tok[0][:], x_tok[j][:], AF.Square, [x_tok[j]], [h_tok[0], stat_n], accum_out=stat_n[:, j:j + 1])
        rstd_from_ss(stat_n[:, 4:8], stat_n[:, 0:4], D, stat_n)
        for j in jl:
            ht = h_tok[j % 2]
            STT(ht[:], x_tok[j][:], stat_n[:, 4 + j:5 + j], rowb[:, w_off:w_off + D], ALU.mult, ALU.mult,
                [x_tok[j], stat_n, rowb], [ht])
            for kc in range(8):
                op("tensor", lambda e, kc=kc, ht=ht: e.transpose(ptr[:, kc * 128:(kc + 1) * 128], ht[:, kc * 128:(kc + 1) * 128], ident[:]),
                   [ht, ident], [ptr], signal=(kc == 7))
            CP(hT[:, :, j * 128:(j + 1) * 128], ptr[:, :].rearrange("p (c t) -> p c t", t=128), [ptr], [hT])

    def fm_mm(b, sl, co, rhsT):
        P.mm(b[:], [(sl[:, kc, co:co + 128], rhsT[:, kc, :]) for kc in range(8)], R=[sl, rhsT], W=[b])

    def conv_steps(b, w, acc, halo, hidx, K, woff, boff, func, out_ap=None, out_tiles=None):
        H = K - 1
        CP(w[:, 0:H], halo[:, hidx, :], [halo], [w], eng="gpsimd")
        ACT(w[:, H:H + T], b[:], AF.Copy, [b], [w])
        ACT(acc[:, 0:T], b[:], AF.Identity, [b, cvec], [acc], scale=cvec[:, woff + K - 1:woff + K], bias=cvec[:, boff:boff + 1])
        banks.put([b])
        CP(halo[:, hidx, :], w[:, T:T + H], [w], [halo], eng="gpsimd")
        yield
        for k in range(K - 1):
            STT(acc[:, 0:T], w[:, k:k + T], cvec[:, woff + k:woff + k + 1], acc[:, 0:T], ALU.mult, ALU.add, [w, cvec, acc], [acc])
            yield
        if func is not None:
            if out_ap is None:
                ACT(acc[:, 0:T], acc[:, 0:T], func, [acc], [acc])
            else:
                ACT(out_ap, acc[:, 0:T], func, [acc], out_tiles)

    def view_bf(t, lo, hi, name):
        v = Tile(t[:, :].bitcast(BF16)[:, lo:hi], name)
        v.buf = t.buf
        return v

    def conv_pe_steps(b, gw, gd, halo, hidx, K, ci):
        H = K - 1
        ubf = view_bf(gw, 0, H + T, "ubf")
        dg = [view_bf(gd, k * 128, (k + 1) * 128, f"dg{k}") for k in range(K)]
        P.dma("sync", gd[:, :].bitcast(BF16)[:, 0:K * 128], dgscr[ci], "d_dg_" + gd.buf.name, R=[dgbuf[ci]], W=[gd], dur=2.5)
        CP(ubf[:, 0:H], halo[:, hidx, :], [halo], [gw], eng="gpsimd")
        ACT(ubf[:, H:H + T], b[:], AF.Copy, [b], [gw])
        banks.put([b])
        CP(halo[:, hidx, :], ubf[:, T:T + H], [gw], [halo], eng="gpsimd")
        yield
        (bc_,) = yield from banks.get(1)
        P.mm(bc_[:], [(dg[k][:], ubf[:, k:k + T]) for k in range(K)], R=[gd, gw], W=[bc_])
        yield
        return bc_

    def z_chain(gh, h, j):
        (sl,) = yield from gh.get()
        (b,) = yield from banks.get(1)
        P.mm(b[:], [(hT[:, kc, j * 128:(j + 1) * 128], sl[:, kc, :]) for kc in range(8)], R=[sl, hT], W=[b])
        gh.release()
        yield
        ACT(siluz[j][:, h * 512:(h + 1) * 512], b[:], AF.Silu, [b], [siluz[j]])
        banks.put([b])

    def xbc_chain(gh, c, ch, mode="F"):
        (sl,) = yield from gh.get()
        if mode == "S" and ch >= 10:
            (b,) = yield from banks.get(1)
            P.mm(b[:, 0:3], [(sl[:, kc, c * 128:(c + 1) * 128], hT[:, kc, T - 3:T]) for kc in range(8)], R=[sl, hT], W=[b])
            gh.release()
            yield
            ACT(halo_s[:, ch, :], b[:, 0:3], AF.Copy, [b], [halo_s])
            banks.put([b])
            return
        gw, gd = yield from gens.get(2)
        (b,) = yield from banks.get(1)
        fm_mm(b, sl, c * 128, hT)
        gh.release()
        yield
        if ch < 8:
            dst, dt_ = xsT[ch][:], [xsT[ch]]
        elif ch < 10:
            dst, dt_ = BT[ch - 8][:], [BT[ch - 8]]
        else:
            dst, dt_ = CT[ch - 10][:], [CT[ch - 10]]
        bc_ = yield from conv_pe_steps(b, gw, gd, halo_s, ch, 4, ch)
        ACT(dst, bc_[:], AF.Silu, [bc_, cvec], dt_, bias=cvec[:, O_CBS + ch:O_CBS + ch + 1])
        banks.put([bc_])
        gens.put([gw, gd])

    def ly_chain(gh, c, ch, cs):
        n = cs.stop - cs.start
        (sl,) = yield from gh.get()
        (b,) = yield from banks.get(1)
        P.mm(b[:, 0:n], [(sl[:, kc, c * 128:(c + 1) * 128], hT[:, kc, cs]) for kc in range(8)], R=[sl, hT], W=[b])
        gh.release()
        yield
        ACT(gy[ch][:, cs], b[:, 0:n], AF.Gelu_apprx_tanh, [b], [gy[ch]])
        banks.put([b])

    def lx_chain(gh, c, ch, mode="F", cs=slice(0, T)):
        (sl,) = yield from gh.get()
        w, xcf, lr, li, la, lm = yield from gens.get(6)
        (b,) = yield from banks.get(1)
        fm_mm(b, sl, c * 128, hT)
        gh.release()
        yield
        bc_ = yield from conv_pe_steps(b, w, lm, halo_l, ch, 4, 12 + ch)
        xcb = view_bf(w, 516, 516 + T, "xcb")
        ACT(xcf[:, 0:T], bc_[:], AF.Identity, [bc_, cvec], [xcf], bias=cvec[:, O_CBL + ch:O_CBL + ch + 1])
        ACT(xcb[:], bc_[:], AF.Identity, [bc_, cvec], [w], bias=cvec[:, O_CBL + ch:O_CBL + ch + 1])
        banks.put([bc_])
        yield
        br_, bi_ = yield from banks.get(2)
        P.mm(br_[:], [(bd[:, ch, :], xcb[:])], R=[bd, xcb], W=[br_])
        P.mm(bi_[:], [(bd[:, 8 + ch, :], xcb[:])], R=[bd, xcb], W=[bi_])
        yield
        ACT(lr[:, 0:T], br_[:], AF.Sigmoid, [br_, cvec], [lr], bias=cvec[:, O_BR + ch:O_BR + ch + 1])
        ACT(li[:, 0:T], bi_[:], AF.Sigmoid, [bi_, cvec], [li], bias=cvec[:, O_BI + ch:O_BI + ch + 1])
        banks.put([br_, bi_])
        yield
        ACT(la[:, 0:T], lr[:, 0:T], AF.Exp, [lr, cst], [la], scale=cst[:, 16 + ch:17 + ch])
        ACT(lm[:, 0:T], lr[:, 0:T], AF.Exp, [lr, cst], [lm], scale=cst[:, 24 + ch:25 + ch])
        TT(li[:, 0:T], li[:, 0:T], xcf[:, 0:T], ALU.mult, [li, xcf], [li], eng="gpsimd")
        yield
        ACT(lm[:, 0:T], lm[:, 0:T], AF.Ln, [lm], [lm], scale=-1.0, bias=1.0)
        ACT(lm[:, 0:T], lm[:, 0:T], AF.Exp, [lm], [lm], scale=0.5)
        yield
        TT(li[:, 0:T], li[:, 0:T], lm[:, 0:T], ALU.mult, [li, lm], [li])
        yield
        op("vector", lambda e: e.tensor_tensor_scan(out=lr[:, 0:T], data0=la[:, 0:T], data1=li[:, 0:T], initial=lru_h[:, ch:ch + 1],
                                                    op0=ALU.mult, op1=ALU.add),
           [la, li, lru_h], [lr])
        yield
        CP(lru_h[:, ch:ch + 1], lr[:, T - 1:T], [lr], [lru_h], eng="gpsimd")
        if mode != "S":
            TT(y_bT[:, ch, cs], lr[:, cs], gy[ch][:, cs], ALU.mult, [lr, gy[ch]], [y_bT])
        gens.put([w, xcf, lr, li, la, lm])

    def ssd_chain(ti, j, mode="F", tmode="F"):
        if True:
            (st,) = yield from ssds.get(1)
            dtb, exb, MT, cbTm, xs_dec, ybuf, ya_tok, xsD, xdt = (st[k] for k in ("dtb", "exb", "MT", "cbTm", "xs_dec", "ybuf", "ya_tok", "xsD", "xdt"))
            ue4 = [st["ue4"]] * 2
            exps = [st["exps"]] * 2
            js = slice(j * 128, (j + 1) * 128)

            def bump():
                P.defer(lambda: state_ver.__setitem__(ti, state_ver.get(ti, 0) + 1))
            (bdt,) = yield from banks.get(1)
            P.mm(bdt[:, 0:16], [(hT[:, kc, js], wdt[:, kc, :]) for kc in range(8)], R=[hT, wdt], W=[bdt])
            TT(dtb[:, 0:16], bdt[:, 0:16], rowb[:, R_DTB:R_DTB + 16], ALU.add, [bdt, rowb], [dtb])
            banks.put([bdt])
            yield
            ACT(dtb[:, 0:16], dtb[:, 0:16], AF.Exp, [dtb], [dtb])
            ACT(dtb[:, 0:16], dtb[:, 0:16], AF.Ln, [dtb], [dtb], bias=1.0)
            TT(dtb[:, 16:32], dtb[:, 0:16], AROW, ALU.mult, [dtb, cst], [dtb])
            yield
            (bcs,) = yield from banks.get(1)
            P.mm(bcs[:, 0:16], [(U, dtb[:, 16:32])], R=[tri, dtb], W=[bcs], signal=False)
            P.mm(bcs[:, 16:32], [(G, dtb[:, 16:32])], R=[tri, dtb], W=[bcs], signal=False)
            P.mm(bcs[:, 32:48], [(ONES, dtb[:, 16:32])], R=[tri, dtb], W=[bcs])
            ACT(exb[:], bcs[:, 0:48], AF.Exp, [bcs], [exb])
            banks.put([bcs])
            yield
            TT(dtb[:, 32:48], dtb[:, 0:16], exb[:, 16:32], ALU.mult, [dtb, exb], [dtb])
            TT(xs_dec[:, :].rearrange("p (e d) -> p e d", d=64), xs_tok[j][:, :].rearrange("p (e d) -> p e d", d=64),
               bc(dtb[:, 32:48], [128, 16, 64]), ALU.mult, [xs_tok[j], dtb], [xs_dec])
            if mode != "S":
                TT(xsD[:, :].rearrange("p (e d) -> p e d", d=64), xs_tok[j][:, :].rearrange("p (e d) -> p e d", d=64),
                   bc(rowb[:, R_D:R_D + 16], [128, 16, 64]), ALU.mult, [xs_tok[j], rowb], [xsD], eng="gpsimd")
                TT(xdt[:, :].rearrange("p (e d) -> p e d", d=64), xs_tok[j][:, :].rearrange("p (e d) -> p e d", d=64),
                   bc(dtb[:, 0:16], [128, 16, 64]), ALU.mult, [xs_tok[j], dtb], [xdt], eng="gpsimd")
            yield
            if mode == "S":
                while state_ver.get(ti, 0) < j:
                    yield "blocked"
                bst = yield from banks.get(2)
                for g in range(2):
                    P.mm(bst[g][:], [(B_tok[j][:, g * 128:(g + 1) * 128], xs_dec[:, g * 512:(g + 1) * 512])],
                         R=[B_tok[j], xs_dec], W=[bst[g]])
                TT(ssm_state[:, :].rearrange("p (e d) -> p e d", d=64), ssm_state[:, :].rearrange("p (e d) -> p e d", d=64),
                   bc(exb[:, 32:48], [128, 16, 64]), ALU.mult, [ssm_state, exb], [ssm_state])
                for g in range(2):
                    gs = slice(g * 512, (g + 1) * 512)
                    TT(ssm_state[:, gs], ssm_state[:, gs], bst[g][:], ALU.add, [ssm_state, bst[g]], [ssm_state])
                banks.put(bst)
                if j == NJ - 1 or tmode == "H":
                    ACT(ssm_state_bf[:], ssm_state[:], AF.Copy, [ssm_state], [ssm_state_bf])
                bump()
                ssds.put([st])
                return
            (bcb,) = yield from banks.get(1)
            for g in range(2):
                P.mm(bcb[:, g * 128:(g + 1) * 128], [(BT[g][:, js], CT[g][:, js])], R=[BT[g], CT[g]], W=[bcb], signal=(g == 1))
            TT(cbTm[:, :, :], bcb[:, 0:256].rearrange("p (g l) -> p g l", l=128), U.unsqueeze(1).broadcast_to([128, 2, 128]),
               ALU.mult, [bcb, tri], [cbTm])
            banks.put([bcb])
            yield
            for q in range(4):
                u4 = ue4[q % 2]
                ex4 = exps[q % 2]
                TT(u4[:, :, :], U.unsqueeze(1).broadcast_to([128, 4, 128]), bc(dtb[:, 16 + q * 4:20 + q * 4], [128, 4, 128]),
                   ALU.mult, [tri, dtb], [u4], eng="gpsimd")
                yield
                (bsg,) = yield from banks.get(1)
                for e4 in range(4):
                    P.mm(bsg[:, e4 * 128:(e4 + 1) * 128], [(G, u4[:, e4, :])], R=[tri, u4], W=[bsg], signal=(e4 == 3))
                ACT(ex4[:, :, :], bsg[:, :].rearrange("p (e l) -> p e l", l=128), AF.Exp, [bsg], [ex4])
                banks.put([bsg])
                yield
                TT(MT[:, q * 4:(q + 1) * 4, :], ex4[:, :, :], cbTm[:, q // 2, :].unsqueeze(1).broadcast_to([128, 4, 128]), ALU.mult,
                   [ex4, cbTm], [MT])
                yield
            while state_ver.get(ti, 0) < j:
                yield "blocked"
            byo = yield from banks.get(2)
            for g in range(2):
                P.mm(byo[g][:], [(CT[g][:, js], ssm_state_bf[:, g * 512:(g + 1) * 512])], R=[CT[g], ssm_state_bf], W=[byo[g]])
            for g in range(2):
                gs = slice(g * 512, (g + 1) * 512)
                TT(ybuf[:, gs].rearrange("p (e d) -> p e d", d=64), byo[g][:, :].rearrange("p (e d) -> p e d", d=64),
                   bc(exb[:, g * 8:(g + 1) * 8], [128, 8, 64]), ALU.mult, [byo[g], exb], [ybuf])
            banks.put(byo)
            yield
            byd = yield from banks.get(2)
            for g in range(2):
                def fn(e, g=g, j=j, byd=byd):
                    e.matmul(byd[g][:], ident[:], xsD[:, g * 512:(g + 1) * 512], start=True, stop=False)
                    inst = None
                    for e8 in range(8):
                        e_ = g * 8 + e8
                        inst = e.matmul(byd[g][:, e8 * 64:(e8 + 1) * 64], MT[:, e_, :], xdt[:, e_ * 64:(e_ + 1) * 64],
                                        start=False, stop=(e8 == 7))
                    return inst
                op("tensor", fn, [MT, xdt, xsD, ident], [byd[g]], dur=0.3 + 8 * 0.1)
            for g in range(2):
                gs = slice(g * 512, (g + 1) * 512)
                TT(ybuf[:, gs], ybuf[:, gs], byd[g][:], ALU.add, [ybuf, byd[g]], [ybuf])
            banks.put(byd)
            yield
            bst = yield from banks.get(2)
            for g in range(2):
                P.mm(bst[g][:], [(B_tok[j][:, g * 128:(g + 1) * 128], xs_dec[:, g * 512:(g + 1) * 512])],
                     R=[B_tok[j], xs_dec], W=[bst[g]])
            TT(ssm_state[:, :].rearrange("p (e d) -> p e d", d=64), ssm_state[:, :].rearrange("p (e d) -> p e d", d=64),
               bc(exb[:, 32:48], [128, 16, 64]), ALU.mult, [ssm_state, exb], [ssm_state])
            for g in range(2):
                gs = slice(g * 512, (g + 1) * 512)
                TT(ssm_state[:, gs], ssm_state[:, gs], bst[g][:], ALU.add, [ssm_state, bst[g]], [ssm_state])
            banks.put(bst)
            ACT(ssm_state_bf[:], ssm_state[:], AF.Copy, [ssm_state], [ssm_state_bf])
            bump()
            yield
            if ti == 0:
                dump(f"y{j}", ybuf[:], [128, 1024], F32, [ybuf])
            sg = stat_g[j]
            TT(ybuf[:], ybuf[:], siluz[j][:], ALU.mult, [ybuf, siluz[j]], [ybuf])
            for g in range(2):
                ACT(junk[:, 0:512], ybuf[:, g * 512:(g + 1) * 512], AF.Square, [ybuf], [junk, sg], accum_out=sg[:, g:g + 1])
            rstd_from_ss(sg[:, 2:4], sg[:, 0:2], 512, sg)
            yield
            for g in range(2):
                gs = slice(g * 512, (g + 1) * 512)
                STT(ya_tok[:, gs], ybuf[:, gs], sg[:, 2 + g:3 + g], rowb[:, R_SN + g * 512:R_SN + (g + 1) * 512], ALU.mult, ALU.mult,
                    [ybuf, sg, rowb], [ya_tok])
            yield
            for kc in range(8):
                op("tensor", lambda e, kc=kc: e.transpose(ptr[:, kc * 128:(kc + 1) * 128], ya_tok[:, kc * 128:(kc + 1) * 128], ident[:]),
                   [ya_tok, ident], [ptr], signal=(kc == 7))
            CP(y_aT[:, :, js], ptr[:, :].rearrange("p (c t) -> p c t", t=128), [ptr], [y_aT])
            ssds.put([st])

    def merge_chain(gh, c, m, cs):
        n = cs.stop - cs.start
        s_p, s_g = yield from gh.get()
        ga, gb, gt, gt2 = yield from gens.get(4)
        b3, b4 = yield from banks.get(2)

        def mmc(b, sl, co, rhsT):
            P.mm(b[:, 0:n], [(sl[:, kc, co:co + 128], rhsT[:, kc, cs]) for kc in range(8)], R=[sl, rhsT], W=[b])
        mmc(b3, s_g, c * 128, hT)
        mmc(b4, s_g, 256 + c * 128, hT)
        yield
        ACT(ga[:, 0:n], b3[:, 0:n], AF.Sigmoid, [b3, cvec], [ga], bias=cvec[:, O_GB + m:O_GB + m + 1])
        ACT(gb[:, 0:n], b4[:, 0:n], AF.Sigmoid, [b4, cvec], [gb], bias=cvec[:, O_GB + 8 + m:O_GB + 9 + m])
        banks.put([b3, b4])
        yield
        b1, b2 = yield from banks.get(2)
        mmc(b1, s_p, c * 128, y_aT)
        mmc(b2, s_p, 256 + c * 128, y_bT)
        gh.release()
        yield
        TT(gt[:, 0:n], ga[:, 0:n], b1[:, 0:n], ALU.mult, [ga, b1], [gt])
        TT(gt2[:, 0:n], gb[:, 0:n], b2[:, 0:n], ALU.mult, [gb, b2], [gt2])
        banks.put([b1, b2])
        yield
        TT(mergedT[:, m, cs], gt[:, 0:n], gt2[:, 0:n], ALU.add, [gt, gt2], [mergedT])
        gens.put([ga, gb, gt, gt2])

    def wout_chain(gh, j):
        js = slice(j * 128, (j + 1) * 128)
        so = stat_o[j]
        s_wo = yield from gh.get()
        yb, yb2 = yield from gens.get(2)
        bo = yield from banks.get(2)
        for h in range(2):
            P.mm(bo[h][:], [(mergedT[:, kc, js], s_wo[h][:, kc, :]) for kc in range(8)], R=[mergedT, s_wo[h]], W=[bo[h]])
        gh.release()
        yield
        for h in range(2):
            ACT(junk[:, 0:512], bo[h][:], AF.Square, [bo[h]], [junk, so], accum_out=so[:, h:h + 1])
        TT(so[:, 2:3], so[:, 0:1], so[:, 1:2], ALU.add, [so], [so])
        rstd_from_ss(so[:, 3:4], so[:, 2:3], D, so)
        yield
        ybs = [yb, yb2]
        for h in range(2):
            hs = slice(h * 512, (h + 1) * 512)
            STT(ybs[h][:, 0:T], bo[h][:], so[:, 3:4], rowb[:, R_PN1 + h * 512:R_PN1 + (h + 1) * 512], ALU.mult, ALU.mult,
                [bo[h], so, rowb], [ybs[h]])
        banks.put(bo)
        yield
        for h in range(2):
            hs = slice(h * 512, (h + 1) * 512)
            TT(x_tok[j][:, hs], x_tok[j][:, hs], ybs[h][:, 0:T], ALU.add, [x_tok[j], ybs[h]], [x_tok[j]])
        gens.put(ybs)

    def ffn_chain(gh, c, ch):
        s_g, s_v = yield from gh.get()
        wg, ag, wv, av = yield from gens.get(4)
        bg_, bv_ = yield from banks.get(2)
        fm_mm(bg_, s_g, c * 128, hT)
        fm_mm(bv_, s_v, c * 128, hT)
        gh.release()
        yield
        g1 = conv_steps(bg_, wg, ag, halo_f, ch, 3, O_CWF + ch * 3, O_CBF + ch, AF.Gelu_apprx_tanh)
        g2 = conv_steps(bv_, wv, av, halo_f, 24 + ch, 3, O_CWF + (24 + ch) * 3, O_CBF + 24 + ch, None)
        alive = [g1, g2]
        while alive:
            for g in list(alive):
                try:
                    next(g)
                except StopIteration:
                    alive.remove(g)
            yield
        TT(fT[:, ch, :], ag[:, 0:T], av[:, 0:T], ALU.mult, [ag, av], [fT], eng="gpsimd")
        gens.put([wg, ag, wv, av])

    def ffn_halo_chain(gh, c, ch):
        s_g, s_v = yield from gh.get()
        (b,) = yield from banks.get(1)
        P.mm(b[:, 0:2], [(s_g[:, kc, c * 128:(c + 1) * 128], hT[:, kc, T - 2:T]) for kc in range(8)], R=[s_g, hT], W=[b], signal=False)
        P.mm(b[:, 2:4], [(s_v[:, kc, c * 128:(c + 1) * 128], hT[:, kc, T - 2:T]) for kc in range(8)], R=[s_v, hT], W=[b])
        gh.release()
        yield
        ACT(halo_f[:, ch, :], b[:, 0:2], AF.Copy, [b], [halo_f])
        ACT(halo_f[:, 24 + ch, :], b[:, 2:4], AF.Copy, [b], [halo_f])
        banks.put([b])

    retire([bdf], ssd_sets[0]["alias"] + ssd_sets[1]["alias"])
    stg = seq(gens.get(2))
    for ci in range(20):
        g_ = stg[ci % 2]
        woff = (O_CWS + ci * 4) if ci < 12 else (O_CWL + (ci - 12) * 4)
        gv = g_[:, :].bitcast(BF16)
        for k in range(4):
            TS(gv[:, k * 128:(k + 1) * 128], ident[:], cvec[:, woff + k:woff + k + 1], None, ALU.mult, None, [ident, cvec], [g_], eng="gpsimd")
        P.dma("sync", dgscr[ci], gv[:, 0:512], f"d_dgs{ci % 2}", R=[g_], W=[dgbuf[ci]])
    gens.put(stg)
    n_out = 0
    for ti in range(NT):
        t0 = ti * T
        mode = modes[ti]
        if mode == "F" and ti > 0 and modes[ti - 1] != "F":
            fl = flag[:, 0:1]
            TS(ssm_state[:], ssm_state[:], fl, None, ALU.mult, None, [ssm_state, flag], [ssm_state])
            ACT(ssm_state_bf[:], ssm_state[:], AF.Copy, [ssm_state], [ssm_state_bf])
            TS(lru_h[:], lru_h[:], fl, None, ALU.mult, None, [lru_h, flag], [lru_h])
            for hl in (halo_s, halo_l, halo_f):
                TS(hl[:], hl[:], fl, None, ALU.mult, None, [hl, flag], [hl])
        retire([fT], siluz + xsT + xs_tok)
        retire(d_tok, gy + [y_bT])
        for j in range(NJ):
            P.dma("sync", x_tok[j][:], dram["x"][t0 + j * 128:t0 + (j + 1) * 128, :], f"d_x{j}", W=[x_tok[j]])
        if mode != "S":
            issue_conversions(len(conv_todo))
        norm_and_transpose(R_W1)

        def p2_chains():
            jz = range(NJ) if mode == "F" else [NJ - 1]
            for h in range(0 if mode == "S" else 2):
                gh = GH(1, len(jz))
                for j in jz:
                    yield z_chain(gh, h, j)
            for h in range(3):
                gh = GH(1, 4)
                for c in range(4):
                    yield xbc_chain(gh, c, h * 4 + c, mode)
        run(p2_chains(), 3)
        for j in range(NJ):
            for c in range(8):
                op("tensor", lambda e, c=c, j=j: e.transpose(ptr[:, c * 128:(c + 1) * 128], xsT[c][:, j * 128:(j + 1) * 128], ident[:]),
                   [xsT[c], ident], [ptr], signal=(c == 7))
            CP(xs_tok[j][:], ptr[:, :], [ptr], [xs_tok[j]])
            for g in range(2):
                op("tensor", lambda e, g=g, j=j: e.transpose(ptr[:, g * 128:(g + 1) * 128], BT[g][:, j * 128:(j + 1) * 128], ident[:]),
                   [BT[g], ident], [ptr], signal=(g == 1))
            CP(B_tok[j][:], ptr[:, 0:256], [ptr], [B_tok[j]])
        if ti == 0 and mode == "F":
            dump("hT", hT[:], [128, 8, T], BF16, [hT])
            dump("siluz0", siluz[0][:], [128, 1024], BF16, [siluz[0]])
            dump("xs_tok0", xs_tok[0][:], [128, 1024], BF16, [xs_tok[0]])
        retire(xsT, [mergedT])

        cs = slice(0, T) if mode == "F" else slice(T - 128, T)

        def p3_chains():
            for h in range(0 if mode == "S" else 2):
                gh = GH(1, 4)
                for c in range(4):
                    yield ly_chain(gh, c, h * 4 + c, cs)
            for h in range(2):
                gh = GH(1, 4)
                for c in range(4):
                    yield lx_chain(gh, c, h * 4 + c, mode, cs)
        smode = lambda j: mode if (mode != "H" or j == NJ - 1) else "S"
        run(p3_chains(), 2, extra=[ssd_chain(ti, j, smode(j), mode) for j in range(NJ)], extra_steps=1)
        if mode == "S":
            issue_conversions(conv_per_tile)
            continue
        if ti == 0:
            dump("y_aT", y_aT[:], [128, 8, T], BF16, [y_aT])
            dump("y_bT", y_bT[:], [128, 8, T], BF16, [y_bT])

        def p4_chains():
            for q in range(4):
                gh = GH(2, 2)
                for c in range(2):
                    yield merge_chain(gh, c, q * 2 + c, cs)
        run(p4_chains(), 3)
        jl_o = range(NJ) if mode == "F" else [NJ - 1]
        gh_wo = GH(2, len(jl_o))
        run((wout_chain(gh_wo, j) for j in jl_o), 2)
        if ti == 0:
            dump("mergedT", mergedT[:], [128, 8, T], BF16, [mergedT])
            for j in range(NJ):
                dump(f"x1_{j}", x_tok[j][:], [128, 1024], F32, [x_tok[j]])

        retire(siluz + [mergedT] + xs_tok, [fT])
        retire(gy + [y_bT], d_tok)
        if mode == "H":
            norm_and_transpose(R_W2, [NJ - 1])

            def p5h_chains():
                for q in range(6):
                    gh = GH(2, 4)
                    for c in range(4):
                        yield ffn_halo_chain(gh, c, q * 4 + c)
            run(p5h_chains(), 3)
            continue
        norm_and_transpose(R_W2)

        def p5_chains():
            for q in range(6):
                gh = GH(2, 4)
                for c in range(4):
                    yield ffn_chain(gh, c, q * 4 + c)
        run(p5_chains(), 3)
        if ti == 0:
            dump("fT", fT[:], [128, 24, T], BF16, [fT])
        for h in range(2):
            bd_ = seq(banks.get(NJ))
            for kb in range(3):
                ghd = GH(1, 1)
                (sl,) = seq(ghd.get())
                for j in range(NJ):
                    js = slice(j * 128, (j + 1) * 128)

                    def fn(e, j=j, kb=kb, sl=sl, js=js, bd_=bd_):
                        inst = None
                        for k8 in range(8):
                            inst = e.matmul(bd_[j][:], fT[:, kb * 8 + k8, js], sl[:, k8, :], start=(kb == 0 and k8 == 0),
                                            stop=(kb == 2 and k8 == 7))
                        return inst
                    op("tensor", fn, [fT, sl], [bd_[j]], signal=True)
                ghd.release()
            for j in range(NJ):
                hs = slice(h * 512, (h + 1) * 512)
                ACT(d_tok[j][:, hs], bd_[j][:], AF.Copy, [bd_[j]], [d_tok[j]])
                ACT(junk[:, 0:512], d_tok[j][:, hs], AF.Square, [d_tok[j]], [junk, stat_f], accum_out=stat_f[:, h * 4 + j:h * 4 + j + 1])
            banks.put(bd_)
        TT(stat_f[:, 8:12], stat_f[:, 0:4], stat_f[:, 4:8], ALU.add, [stat_f], [stat_f])
        rstd_from_ss(stat_f[:, 12:16], stat_f[:, 8:12], D, stat_f)
        for j in range(NJ):
            STT(d_tok[j][:], d_tok[j][:], stat_f[:, 12 + j:13 + j], rowb[:, R_PN2:R_PN2 + D], ALU.mult, ALU.mult,
                [d_tok[j], stat_f, rowb], [d_tok[j]])
            TT(x_tok[j][:], x_tok[j][:], d_tok[j][:], ALU.add, [x_tok[j], d_tok[j]], [x_tok[j]])
            P.dma("sync", d_out[n_out * T + j * 128:n_out * T + (j + 1) * 128, :], x_tok[j][:], f"d_o{j}", R=[x_tok[j]])
        n_out += 1

    P.wait_all("sync", [(f"d_o{j}", P.cnt[f"d_o{j}"]) for j in range(NJ)] + dbg_toks)
    P.emit()
    return nc


def prep_shared(inp):
    f = np.float32
    g = lambda k: np.asarray(inp[k], dtype=f)[0]
    sh = {}
    sh["w_in"] = np.ascontiguousarray(g("w_in"))
    sh["p_a"] = np.ascontiguousarray(g("w_proj_ssm"))
    sh["p_b"] = np.ascontiguousarray(g("w_proj_lru"))
    sh["w_out"] = np.ascontiguousarray(g("w_out"))
    sh["w_up"] = np.ascontiguousarray(g("w_ffn_up"))
    sh["w_down"] = np.ascontiguousarray(g("w_ffn_down"))
    bd = np.zeros((128, 16, 128), f)
    for gi, k in enumerate(("lru_wr", "lru_wi")):
        w = g(k)
        for ch in range(8):
            for hl in range(2):
                bd[hl * 64:(hl + 1) * 64, gi * 8 + ch, hl * 64:(hl + 1) * 64] = w[ch * 2 + hl]
    sh["bd"] = bd

    def cols(v):
        return np.ascontiguousarray(v.reshape(-1, 128).T)
    cv = np.zeros((128, NCV), f)
    cws = g("ssm_conv_w")
    for k in range(4):
        cv[:, O_CWS + k:O_CWS + 48:4] = cols(cws[k])
    cv[:, O_CBS:O_CBS + 12] = cols(g("ssm_conv_b"))
    cwl = g("lru_conv_w")
    for k in range(4):
        cv[:, O_CWL + k:O_CWL + 32:4] = cols(cwl[k])
    cv[:, O_CBL:O_CBL + 8] = cols(g("lru_conv_b"))
    cv[:, O_BR:O_BR + 8] = cols(g("lru_br").reshape(-1))
    cv[:, O_BI:O_BI + 8] = cols(g("lru_bi").reshape(-1))
    cv[:, O_LAM:O_LAM + 8] = cols(g("lru_lambda"))
    gbv = g("gate_b")
    cv[:, O_GB:O_GB + 8] = cols(gbv[0])
    cv[:, O_GB + 8:O_GB + 16] = cols(gbv[1])
    cwf = g("ffn_conv_w")
    for k in range(3):
        cv[:, O_CWF + k:O_CWF + 144:3] = cols(cwf[k])
    cv[:, O_CBF:O_CBF + 48] = cols(g("ffn_conv_b"))
    sh["cvec"] = cv
    rb = np.zeros((128, NRB), f)
    rb[:, R_PN1:R_PN1 + D] = g("mix_post_norm")[None]
    rb[:, R_PN2:R_PN2 + D] = g("ffn_post_norm")[None]
    rb[:, R_SN:R_SN + D] = g("ssm_norm")[None]
    rb[:, R_W1:R_W1 + D] = g("mix_pre_norm")[None]
    rb[:, R_W2:R_W2 + D] = g("ffn_pre_norm")[None]
    rb[:, R_DTB:R_DTB + 16] = g("ssm_dt_bias")[None]
    rb[:, R_ALOG:R_ALOG + 16] = g("ssm_a_log")[None]
    rb[:, R_D:R_D + 16] = g("ssm_d")[None]
    sh["rowb"] = rb
    sh["ident"] = np.eye(128).astype(ml_dtypes.bfloat16)
    k = np.arange(128)
    tri = np.zeros((128, 3, 128), f)
    tri[:, 0, :] = (k[:, None] <= k[None, :])
    tri[:, 1, :] = (k[:, None] > k[None, :])
    tri[:, 2, :] = 1.0
    sh["tri"] = tri
    return sh


PREFIX_MODES = ["S", "S", "S", "H"]


def kernel(**inputs):
    x = np.asarray(inputs["x"], dtype=np.float32)
    B, S, _ = x.shape
    sh = prep_shared(inputs)
    half = S // 2
    n_own = half // T
    modes = PREFIX_MODES[-(half // T):] + ["F"] * n_own
    nc = build_program(modes)
    in_maps = []
    for c in range(8):
        b, hf = c // 2, c % 2
        m = dict(sh)
        pre = x[b, :half] if hf else np.zeros((half, D), np.float32)
        m["x"] = np.ascontiguousarray(np.concatenate([pre, x[b, hf * half:(hf + 1) * half]], axis=0))
        m["flag"] = np.full((128, 1), float(hf), np.float32)
        in_maps.append(m)
    res = run_bass_kernel_spmd(nc, in_maps, core_ids=list(range(8)))
    out = np.empty((B, S, D), np.float32)
    for c in range(8):
        b, hf = c // 2, c % 2
        out[b, hf * half:(hf + 1) * half] = res.results[c]["out"]
    return out
```

```python
import contextlib
import numpy as np
import ml_dtypes
import concourse.bass as bass
import concourse.mybir as mybir
from concourse.bass_utils import run_bass_kernel_spmd

F32 = mybir.dt.float32
BF16 = mybir.dt.bfloat16
AF = mybir.ActivationFunctionType
ALU = mybir.AluOpType
AX = mybir.AxisListType

ENGS = ("sync", "scalar", "vector", "gpsimd", "tensor")


class Buf:
    __slots__ = ("name", "w", "r", "tw", "tr")

    def __init__(self, name):
        self.name = name
        self.w = None
        self.r = []
        self.tw = 0.0
        self.tr = 0.0


class Tile:
    def __init__(self, t, name):
        self.t = t
        self.buf = Buf(name)

    def __getitem__(self, k):
        return self.t[k]


class Prog:
    def __init__(self, nc):
        self.nc = nc
        self.es = contextlib.ExitStack()
        self.q = {e: [] for e in ENGS}
        self.sem = {}
        self.cnt = {}
        self.seen = {e: {} for e in ENGS}
        for e in ENGS:
            self.newsem("E_" + e)
        self.n_t = 0
        self.fifo = None
        self.act_set = None
        self.eng_free = {e: 0.0 for e in ENGS}

    def newsem(self, name):
        if name not in self.sem:
            self.sem[name] = self.es.enter_context(self.nc.semaphore(name))
            self.cnt[name] = 0
        return name

    def sb(self, shape, dtype, name=None):
        self.n_t += 1
        name = "s_" + (name or f"t{self.n_t}")
        t = self.es.enter_context(self.nc.sbuf_tensor(name, list(shape), dtype))
        return Tile(t, name)

    def ps(self, shape, dtype, name=None):
        self.n_t += 1
        name = "ps_" + (name or f"p{self.n_t}")
        t = self.es.enter_context(self.nc.psum_tensor(name, list(shape), dtype))
        return Tile(t, name)

    def _deps(self, eng, R, W):
        need = {}
        for b in R:
            b = b.buf if isinstance(b, Tile) else b
            if b.w is not None:
                s, v = b.w
                need[s] = max(need.get(s, 0), v)
        for b in W:
            b = b.buf if isinstance(b, Tile) else b
            if b.w is not None:
                s, v = b.w
                need[s] = max(need.get(s, 0), v)
            for s, v in b.r:
                need[s] = max(need.get(s, 0), v)
        waits = []
        seen = self.seen[eng]
        for s, v in need.items():
            if eng == "tensor" and s == "E_tensor":
                continue
            if seen.get(s, 0) < v:
                seen[s] = v
                waits.append((s, v))
        return waits

    def _mark(self, tok, R, W):
        for b in R:
            b = b.buf if isinstance(b, Tile) else b
            b.r.append(tok)
            if len(b.r) > 24:
                m = {}
                for s, v in b.r:
                    m[s] = max(m.get(s, 0), v)
                b.r = list(m.items())
        for b in W:
            b = b.buf if isinstance(b, Tile) else b
            b.w = tok
            b.r = []

    @staticmethod
    def _bufs(xs):
        return tuple(b.buf if isinstance(b, Tile) else b for b in xs)

    def op(self, eng, fn, R=(), W=(), signal=True, dur=0.5, aset=None):
        d = ("op", eng, fn, self._bufs(R), self._bufs(W), signal, dur, aset)
        if self.fifo is not None:
            self.fifo.append(d)
        else:
            self._emit(d)

    def dma(self, queue, out, in_, sem, R=(), W=(), dur=8.0):
        self.newsem(sem)
        d = ("dma", queue, (out, in_, sem), self._bufs(R), self._bufs(W), True, dur, None)
        if self.fifo is not None:
            self.fifo.append(d)
            return (sem, self.cnt[sem] + 16)
        return self._emit(d)

    def defer(self, cb):
        if self.fifo is not None:
            self.fifo.append(("cb", cb))
        else:
            cb()

    def est_start(self, d):
        _, eng, _, R, W, _, _, aset = d
        t = self.eng_free[eng]
        for b in R:
            t = max(t, b.tw)
        for b in W:
            t = max(t, b.tw, b.tr)
        if aset is not None and aset != self.act_set:
            t += 2.5
        return t

    def _emit(self, d):
        kind, eng, fn, R, W, signal, dur, aset = d
        start = self.est_start(d)
        if aset is not None:
            self.act_set = aset
        waits = self._deps(eng, R, W)
        if kind == "dma":
            out, in_, sem = fn
            self.cnt[sem] += 16
            tok = (sem, self.cnt[sem])
            self.q[eng].append((waits, lambda e: e.dma_start(out=out, in_=in_), (sem, 16)))
            self.eng_free[eng] = start + 1.0
            fin = start + dur
        else:
            sname = "E_" + eng
            if signal:
                self.cnt[sname] += 1
                tok = (sname, self.cnt[sname])
            else:
                tok = (sname, self.cnt[sname] + 1)
            self.q[eng].append((waits, fn, (sname, 1) if signal else None))
            fin = start + dur
            self.eng_free[eng] = fin
        fin_vis = fin + 0.15
        for b in R:
            b.tr = max(b.tr, fin_vis)
        for b in W:
            b.tw = fin_vis
            b.tr = 0.0
        self._mark(tok, R, W)
        return tok

    def mm(self, out, pairs, R=(), W=(), signal=True):
        n = len(pairs)

        def fn(e):
            inst = None
            for i, (l, r) in enumerate(pairs):
                inst = e.matmul(out, l, r, start=(i == 0), stop=(i == n - 1))
            return inst
        try:
            nfree = pairs[0][1].free_size()
        except Exception:
            nfree = 512
        return self.op("tensor", fn, R, W, signal, dur=n * (0.07 + nfree / 1900.0))

    def wait_all(self, eng, toks):
        waits = []
        for s, v in toks:
            if self.seen[eng].get(s, 0) < v:
                self.seen[eng][s] = v
                waits.append((s, v))
        self.q[eng].append((waits, None, None))

    def emit(self):
        nc = self.nc
        with nc.Block() as block:
            def make(eng_name):
                def body(eng):
                    for waits, fn, inc in self.q[eng_name]:
                        for s, v in waits:
                            eng.wait_ge(self.sem[s], v)
                        if fn is not None:
                            inst = fn(eng)
                            if inc is not None:
                                inst.then_inc(self.sem[inc[0]], inc[1])
                return body
            for e in ENGS:
                if self.q[e]:
                    getattr(block, e)(make(e))
        self.es.close()


D = 1024
NIN = 6672
FF = 3072
T = 512
NJ = T // 128
EPS = 1e-6

O_CWS, O_CBS, O_CWL, O_CBL, O_BR, O_BI, O_LAM, O_GB, O_CWF, O_CBF, NCV = 0, 48, 60, 92, 100, 108, 116, 124, 140, 284, 332
R_PN1, R_PN2, R_SN, R_W1, R_W2, R_DTB, R_ALOG, R_D, NRB = 0, 1024, 2048, 3072, 4096, 5120, 5136, 5152, 5168

def _groups(mode="F"):
    g = []
    if mode != "S":
        for h in range(2):
            g.append(("w_in", 0, h * 512))
    for h in range(3):
        g.append(("w_in", 0, 1024 + h * 512))
    if mode != "S":
        for h in range(2):
            g.append(("w_in", 0, 2576 + h * 512))
    for h in range(2):
        g.append(("w_in", 0, 3600 + h * 512))
    if mode == "S":
        return g
    for q in range(4):
        g.append([("p_a", 0, q * 256, 256, 0), ("p_b", 0, q * 256, 256, 256)])
        g.append([("w_in", 0, 4624 + q * 256, 256, 0), ("w_in", 0, 4624 + 1024 + q * 256, 256, 256)])
    for h in range(2):
        g.append(("w_out", 0, h * 512))
    for q in range(6):
        g.append(("w_up", 0, q * 512))
        g.append(("w_up", 0, 3072 + q * 512))
    if mode == "H":
        return g
    for h in range(2):
        for kb in range(3):
            g.append(("w_down", kb, h * 512))
    return g


NSLOT = 3


def build_program(modes, dbg=False):
    nc = bass.Bass("TRN2", target_bir_lowering=False)
    if isinstance(modes, int):
        modes = ["F"] * modes
    NT = len(modes)
    S = NT * T
    NF = sum(1 for m in modes if m == "F")
    ALLG = _groups("F")
    key = lambda g: repr(g)
    gid_of = {key(g): i for i, g in enumerate(ALLG)}
    GROUPS = []
    for m in modes:
        GROUPS += [gid_of[key(g)] for g in _groups(m)]
    dram = {}
    dram["x"] = nc.dram_tensor("x", [S, D], F32, kind="ExternalInput").ap()
    dram["w_in"] = nc.dram_tensor("w_in", [D, NIN], F32, kind="ExternalInput").ap()
    dram["p_a"] = nc.dram_tensor("p_a", [D, D], F32, kind="ExternalInput").ap()
    dram["p_b"] = nc.dram_tensor("p_b", [D, D], F32, kind="ExternalInput").ap()
    dram["w_out"] = nc.dram_tensor("w_out", [D, D], F32, kind="ExternalInput").ap()
    dram["w_up"] = nc.dram_tensor("w_up", [D, 2 * FF], F32, kind="ExternalInput").ap()
    dram["w_down"] = nc.dram_tensor("w_down", [FF, D], F32, kind="ExternalInput").ap()
    d_bd = nc.dram_tensor("bd", [128, 16, 128], F32, kind="ExternalInput").ap()
    d_cvec = nc.dram_tensor("cvec", [128, NCV], F32, kind="ExternalInput").ap()
    d_rowb = nc.dram_tensor("rowb", [128, NRB], F32, kind="ExternalInput").ap()
    d_ident = nc.dram_tensor("ident", [128, 128], BF16, kind="ExternalInput").ap()
    d_tri = nc.dram_tensor("tri", [128, 3, 128], F32, kind="ExternalInput").ap()
    d_out = nc.dram_tensor("out", [NF * T, D], F32, kind="ExternalOutput").ap()
    d_flag = nc.dram_tensor("flag", [128, 1], F32, kind="ExternalInput").ap()

    P = Prog(nc)
    op = P.op
    dbg_toks = []

    def dump(name, ap, shape, dtype, R):
        if not dbg:
            return
        d = nc.dram_tensor("dbg_" + name, list(shape), dtype, kind="ExternalOutput").ap()
        dbg_toks.append(P.dma("sync", d, ap, "d_dbg_" + name, R=R))

    def vdur(out, eng="vector"):
        n = out.free_size()
        return (0.12 + n / 960.0) if eng == "vector" else (0.25 + n / 450.0)

    ASET = {AF.Exp: "ln_exp", AF.Ln: "ln_exp", AF.Sigmoid: "sig", AF.Silu: "silu", AF.Gelu_apprx_tanh: "gelu"}

    def ACT(out, in_, func, R, W, **kw):
        return op("scalar", lambda e: e.activation(out=out, in_=in_, func=func, **kw), R, W, dur=0.22 + out.free_size() / 1200.0,
                  aset=ASET.get(func))

    def TT(out, a, b, alu, R, W, eng="vector"):
        return op(eng, lambda e: e.tensor_tensor(out=out, in0=a, in1=b, op=alu), R, W, dur=vdur(out, eng))

    def STT(out, in0, scalar, in1, op0, op1, R, W):
        return op("vector", lambda e: e.scalar_tensor_tensor(out=out, in0=in0, scalar=scalar, in1=in1, op0=op0, op1=op1), R, W,
                  dur=vdur(out))

    def TS(out, in0, s1, s2, op0, op1, R, W, eng="vector"):
        if s2 is None:
            return op(eng, lambda e: e.tensor_scalar(out=out, in0=in0, scalar1=s1, scalar2=None, op0=op0), R, W, dur=vdur(out, eng))
        return op(eng, lambda e: e.tensor_scalar(out=out, in0=in0, scalar1=s1, scalar2=s2, op0=op0, op1=op1), R, W, dur=vdur(out, eng))

    def CP(out, in_, R, W, eng="vector"):
        return op(eng, lambda e: e.tensor_copy(out=out, in_=in_), R, W, dur=vdur(out, eng))

    def bc(ap, shape):
        return ap.unsqueeze(2).broadcast_to(shape)

    cvec = P.sb([128, NCV], F32, "cvec")
    rowb = P.sb([128, NRB], F32, "rowb")
    ident = P.sb([128, 128], BF16, "ident")
    tri = P.sb([128, 3, 128], F32, "tri")
    bdf = P.sb([128, 16, 128], F32, "bdf")
    bd = P.sb([128, 16, 128], BF16, "bd")
    wdt = P.sb([128, 8, 16], BF16, "wdt")
    flag = P.sb([128, 1], F32, "flag")
    consts = [cvec, rowb, ident, tri, bdf, flag]
    P.dma("sync", flag[:], d_flag, "d_const", W=[flag])
    P.dma("sync", cvec[:], d_cvec, "d_const", W=[cvec])
    P.dma("sync", rowb[:], d_rowb, "d_const", W=[rowb])
    P.dma("sync", ident[:], d_ident, "d_const", W=[ident])
    P.dma("sync", tri[:], d_tri, "d_const", W=[tri])
    P.dma("sync", bdf[:], d_bd, "d_const", W=[bdf])
    for c in consts:
        c.buf.w = ("d_const", P.cnt["d_const"])
    P.dma("gpsimd", wdt[:], dram["w_in"].rearrange("(kc p) c -> p kc c", p=128)[:, :, 2560:2576], "d_wdt", W=[wdt])
    CP(bd[:], bdf[:], [bdf], [bd])
    U = tri[:, 0, :]
    G = tri[:, 1, :]
    ONES = tri[:, 2, :]
    cst = P.sb([128, 64], F32, "cst")
    ACT(cst[:, 0:16], rowb[:, R_ALOG:R_ALOG + 16], AF.Exp, [rowb], [cst])
    TS(cst[:, 0:16], cst[:, 0:16], -1.0, None, ALU.mult, None, [cst], [cst])
    ACT(cst[:, 32:40], cvec[:, O_LAM:O_LAM + 8], AF.Exp, [cvec], [cst], scale=-1.0)
    ACT(cst[:, 32:40], cst[:, 32:40], AF.Ln, [cst], [cst], bias=1.0)
    TS(cst[:, 16:24], cst[:, 32:40], -8.0, None, ALU.mult, None, [cst], [cst])
    TS(cst[:, 24:32], cst[:, 32:40], -16.0, None, ALU.mult, None, [cst], [cst])
    AROW = cst[:, 0:16]

    ssm_state = P.sb([128, 1024], F32, "ssm_state")
    ssm_state_bf = P.sb([128, 1024], BF16, "ssm_state_bf")
    lru_h = P.sb([128, 8], F32, "lru_h")
    halo_s = P.sb([128, 12, 3], F32, "halo_s")
    halo_l = P.sb([128, 8, 3], F32, "halo_l")
    halo_f = P.sb([128, 48, 2], F32, "halo_f")
    op("gpsimd", lambda e: e.memset(ssm_state[:], 0.0), [], [ssm_state])
    op("gpsimd", lambda e: e.memset(ssm_state_bf[:], 0.0), [], [ssm_state_bf])
    op("gpsimd", lambda e: e.memset(lru_h[:], 0.0), [], [lru_h])
    op("gpsimd", lambda e: e.memset(halo_s[:], 0.0), [], [halo_s])
    op("gpsimd", lambda e: e.memset(halo_l[:], 0.0), [], [halo_l])
    op("gpsimd", lambda e: e.memset(halo_f[:], 0.0), [], [halo_f])

    slots = [P.sb([128, 8, 512], BF16, f"slot{i}") for i in range(NSLOT)]
    ring = {"issued": 0, "res": 0, "done": 0}
    need = {}
    total_groups = len(GROUPS)

    wscr = nc.dram_tensor("wscr", [len(ALLG), 128, 8 * 512], BF16).ap()
    cvbuf = [Buf(f"cv{i}") for i in range(len(ALLG))]
    first_use = []
    for g in GROUPS:
        if g not in first_use:
            first_use.append(g)
    dgscr = nc.dram_tensor("dgscr", [20, 128, 512], BF16).ap()
    dgbuf = [Buf(f"dg{i}") for i in range(20)]
    conv_todo = list(first_use)

    def issue_conversions(k, only=None):
        for _ in range(min(k, len(conv_todo))):
            if only is not None:
                if only not in conv_todo:
                    return
                conv_todo.remove(only)
                gid = only
            else:
                gid = conv_todo.pop(0)
            parts = ALLG[gid]
            if isinstance(parts, tuple):
                parts = [parts + (512, 0)]
            for name, rb, c0, n, off in parts:
                src = dram[name][rb * 1024:(rb + 1) * 1024, :].rearrange("(kc p) c -> p kc c", p=128)[:, :, c0:c0 + n]
                dst = wscr[gid].rearrange("p (kc c) -> p kc c", c=512)[:, :, off:off + n]
                P.dma("gpsimd", dst, src, f"d_cv{gid}", W=[cvbuf[gid]], dur=12.0)

    n_first = len(_groups(modes[0]))
    issue_conversions(n_first)
    n_S = sum(1 for m in modes if m == "S")
    conv_per_tile = -(-(len(first_use) - n_first) // max(n_S, 1))

    def issue_load():
        gi = ring["issued"]
        gid = GROUPS[gi]
        issue_conversions(1, only=gid)
        sl = slots[gi % NSLOT]
        P.dma("sync", sl[:], wscr[gid].rearrange("p (kc c) -> p kc c", c=512), f"d_slot{gi % NSLOT}", R=[cvbuf[gid]], W=[sl], dur=5.0)
        ring["issued"] += 1

    def pump():
        while ring["done"] < ring["res"] and need.get(ring["done"], 1) == 0:
            ring["done"] += 1
        while ring["issued"] < total_groups and ring["issued"] < ring["done"] + NSLOT:
            issue_load()

    class GH:
        def __init__(self, n=1, readers=1):
            self.n, self.readers, self.first = n, readers, None

        def get(self):
            if self.first is None:
                self.first = ring["res"]
                ring["res"] += self.n
                for g in range(self.first, self.first + self.n):
                    need[g] = self.readers
            k = 0
            while True:
                pump()
                if ring["issued"] >= self.first + self.n:
                    break
                k += 1
                assert k < 200000, "ring deadlock"
                yield "blocked"
            return [slots[(self.first + i) % NSLOT] for i in range(self.n)]

        def release(self):
            def cb():
                for g in range(self.first, self.first + self.n):
                    need[g] -= 1
                pump()
            P.defer(cb)

    class RPool:
        def __init__(self, items):
            self.free = list(items)

        def get(self, n=1):
            k = 0
            while len(self.free) < n:
                k += 1
                assert k < 200000, "resource deadlock"
                yield "blocked"
            out = self.free[:n]
            del self.free[:n]
            return out

        def put(self, xs):
            xs = list(xs)
            P.defer(lambda: self.free.extend(xs))

    def seq(gen):
        try:
            while True:
                next(gen)
        except StopIteration as e:
            return e.value

    class Chain:
        def __init__(self, gen):
            self.gen, self.fifo, self.done = gen, [], False

    def advance(c):
        while not c.fifo and not c.done:
            P.fifo = c.fifo
            try:
                r = next(c.gen)
            except StopIteration:
                c.done = True
                r = None
            P.fifo = None
            if r == "blocked":
                return

    def run(chains, W, extra=(), extra_steps=1):
        act = [Chain(g) for g in extra]
        for c in act:
            c.extra = True
        it = iter(chains)
        pending = True
        nwin = 0
        guard = 0
        while True:
            while pending and sum(1 for c in act if not getattr(c, "extra", False)) < W:
                try:
                    act.append(Chain(next(it)))
                except StopIteration:
                    pending = False
            progressed = False
            for c in act:
                if not c.fifo and not c.done:
                    advance(c)
                while c.fifo and c.fifo[0][0] == "cb":
                    c.fifo.pop(0)[1]()
                    progressed = True
            before = len(act)
            act = [c for c in act if c.fifo or not c.done]
            if len(act) != before:
                progressed = True
            cands = [c for c in act if c.fifo]
            if not cands:
                if not act and not pending:
                    break
                guard += 1
                assert progressed or guard < 100000, "scheduler deadlock"
                continue
            guard = 0
            best = min(cands, key=lambda c: P.est_start(c.fifo[0]))
            P._emit(best.fifo.pop(0))

    ptr = P.ps([128, 1024], BF16, "ptr")
    banks = RPool([P.ps([128, 512], F32, f"bank{i}") for i in range(7)])

    x_tok = [P.sb([128, 1024], F32, f"x_tok{j}") for j in range(NJ)]
    h_tok = [P.sb([128, 1024], BF16, "h_tok0")] * 2
    junk = P.sb([128, 512], BF16, "junk")
    hT = P.sb([128, 8, T], BF16, "hT")
    stat_n = P.sb([128, 8], F32, "stat_n")
    stat_g = [P.sb([128, 4], F32, f"stat_g{j}") for j in range(NJ)]
    stat_o = [P.sb([128, 4], F32, f"stat_o{j}") for j in range(NJ)]
    stat_f = P.sb([128, 16], F32, "stat_f")
    regA = P.sb([128, 24 * T], BF16, "regA")
    siluz = [Tile(regA[:, j * 1024:(j + 1) * 1024], f"siluz{j}") for j in range(NJ)]
    xsT = [Tile(regA[:, 4096 + c * T: 4096 + (c + 1) * T], f"xsT{c}") for c in range(8)]
    mergedT = Tile(regA[:, 4096:8192].rearrange("p (c t) -> p c t", t=T), "mergedT")
    xs_tok = [Tile(regA[:, 8192 + j * 1024: 8192 + (j + 1) * 1024], f"xs_tok{j}") for j in range(NJ)]
    fT = Tile(regA[:, :].rearrange("p (c t) -> p c t", t=T), "fT")
    regB = P.sb([128, 4 * 1024], F32, "regB")
    regB_bf = regB[:, :].bitcast(BF16)
    gy = [Tile(regB_bf[:, c * T:(c + 1) * T], f"gy{c}") for c in range(8)]
    y_bT = Tile(regB_bf[:, 4096:8192].rearrange("p (c t) -> p c t", t=T), "y_bT")
    d_tok = [Tile(regB[:, j * 1024:(j + 1) * 1024], f"d_tok{j}") for j in range(NJ)]
    BT = [P.sb([128, T], BF16, f"BT{g}") for g in range(2)]
    CT = [P.sb([128, T], BF16, f"CT{g}") for g in range(2)]
    B_tok = [P.sb([128, 256], BF16, f"B_tok{j}") for j in range(NJ)]
    y_aT = P.sb([128, 8, T], BF16, "y_aT")
    NGEN = 12
    gens = RPool([P.sb([128, 3 + T], F32, f"gen{i}") for i in range(NGEN)])
    bdf_flat = bdf[:, :, :].rearrange("p a b -> p (a b)")
    ssd_sets = []
    for i in range(2):
        st = {}
        st["dtb"] = P.sb([128, 64], F32, f"dtb{i}")
        st["exb"] = P.sb([128, 48], F32, f"exb{i}")
        st["ue4"] = P.sb([128, 4, 128], F32, f"ue4_{i}")
        st["exps"] = P.sb([128, 4, 128], F32, f"exps{i}")
        st["MT"] = P.sb([128, 16, 128], BF16, f"MT{i}")
        st["cbTm"] = P.sb([128, 2, 128], F32, f"cbTm{i}")
        st["ybuf"] = P.sb([128, 1024], F32, f"ybuf{i}")
        st["xdt"] = P.sb([128, 1024], BF16, f"xdt{i}")
        if i == 0:
            st["xs_dec"] = P.sb([128, 1024], BF16, "xs_dec0")
            st["ya_tok"] = P.sb([128, 1024], BF16, "ya_tok0")
            st["xsD"] = Tile(bdf_flat[:, 0:512].bitcast(BF16), "xsD0")
            st["alias"] = [st["xsD"]]
        else:
            st["xsD"] = Tile(bdf_flat[:, 512:1024].bitcast(BF16), "xsD1")
            st["xs_dec"] = Tile(bdf_flat[:, 1024:1536].bitcast(BF16), "xs_dec1")
            st["ya_tok"] = Tile(bdf_flat[:, 1536:2048].bitcast(BF16), "ya_tok1")
            st["alias"] = [st["xsD"], st["xs_dec"], st["ya_tok"]]
        ssd_sets.append(st)
    ssds = RPool(ssd_sets)
    state_ver = {}
    eps_t = P.sb([128, 1], F32, "eps_t")
    op("gpsimd", lambda e: e.memset(eps_t[:], EPS), [], [eps_t])

    def retire(olds, news):
        m = {}
        for o in olds:
            toks = list(o.buf.r)
            if o.buf.w is not None:
                toks.append(o.buf.w)
            for s, v in toks:
                m[s] = max(m.get(s, 0), v)
        for n in news:
            n.buf.w = None
            n.buf.r = list(m.items())

    def rstd_from_ss(out_ap, ss_ap, n, tl):
        ACT(out_ap, ss_ap, AF.Ln, [tl, eps_t], [tl], scale=1.0 / n, bias=eps_t[:, 0:1])
        ACT(out_ap, out_ap, AF.Exp, [tl], [tl], scale=-0.5)

    def norm_and_transpose(w_off, jl=range(NJ)):
        if len(jl) < NJ:
            op("gpsimd", lambda e: e.memset(stat_n[:, 0:4], 1.0), [], [stat_n])
        for j in jl:
            ACT(h_tok[0][:], x_tok[j][:], AF.Square, [x_tok[j]], [h_tok[0], stat_n], accum_out=stat_n[:, j:j + 1])
        rstd_from_ss(stat_n[:, 4:8], stat_n[:, 0:4], D, stat_n)
        for j in jl:
            ht = h_tok[j % 2]
            STT(ht[:], x_tok[j][:], stat_n[:, 4 + j:5 + j], rowb[:, w_off:w_off + D], ALU.mult, ALU.mult,
                [x_tok[j], stat_n, rowb], [ht])
            for kc in range(8):
                op("tensor", lambda e, kc=kc, ht=ht: e.transpose(ptr[:, kc * 128:(kc + 1) * 128], ht[:, kc * 128:(kc + 1) * 128], ident[:]),
                   [ht, ident], [ptr], signal=(kc == 7))
            CP(hT[:, :, j * 128:(j + 1) * 128], ptr[:, :].rearrange("p (c t) -> p c t", t=128), [ptr], [hT])

    def fm_mm(b, sl, co, rhsT):
        P.mm(b[:], [(sl[:, kc, co:co + 128], rhsT[:, kc, :]) for kc in range(8)], R=[sl, rhsT], W=[b])

    def conv_steps(b, w, acc, halo, hidx, K, woff, boff, func, out_ap=None, out_tiles=None):
        H = K - 1
        CP(w[:, 0:H], halo[:, hidx, :], [halo], [w], eng="gpsimd")
        ACT(w[:, H:H + T], b[:], AF.Copy, [b], [w])
        ACT(acc[:, 0:T], b[:], AF.Identity, [b, cvec], [acc], scale=cvec[:, woff + K - 1:woff + K], bias=cvec[:, boff:boff + 1])
        banks.put([b])
        CP(halo[:, hidx, :], w[:, T:T + H], [w], [halo], eng="gpsimd")
        yield
        for k in range(K - 1):
            STT(acc[:, 0:T], w[:, k:k + T], cvec[:, woff + k:woff + k + 1], acc[:, 0:T], ALU.mult, ALU.add, [w, cvec, acc], [acc])
            yield
        if func is not None:
            if out_ap is None:
                ACT(acc[:, 0:T], acc[:, 0:T], func, [acc], [acc])
            else:
                ACT(out_ap, acc[:, 0:T], func, [acc], out_tiles)

    def view_bf(t, lo, hi, name):
        v = Tile(t[:, :].bitcast(BF16)[:, lo:hi], name)
        v.buf = t.buf
        return v

    def issue_dg(gd, ci, K=4):
        P.dma("scalar", gd[:, :].bitcast(BF16)[:, 0:K * 128], dgscr[ci], "d_dg_" + gd.buf.name, R=[dgbuf[ci]], W=[gd], dur=2.5)

    def conv_pe_steps(b, gw, gd, halo, hidx, K, ci):
        H = K - 1
        ubf = view_bf(gw, 0, H + T, "ubf")
        dg = [view_bf(gd, k * 128, (k + 1) * 128, f"dg{k}") for k in range(K)]
        CP(ubf[:, 0:H], halo[:, hidx, :], [halo], [gw], eng="gpsimd")
        ACT(ubf[:, H:H + T], b[:], AF.Copy, [b], [gw])
        banks.put([b])
        CP(halo[:, hidx, :], ubf[:, T:T + H], [gw], [halo], eng="gpsimd")
        yield
        (bc_,) = yield from banks.get(1)
        P.mm(bc_[:], [(dg[k][:], ubf[:, k:k + T]) for k in range(K)], R=[gd, gw], W=[bc_])
        yield
        return bc_

    def z_chain(gh, h, j):
        (sl,) = yield from gh.get()
        (b,) = yield from banks.get(1)
        P.mm(b[:], [(hT[:, kc, j * 128:(j + 1) * 128], sl[:, kc, :]) for kc in range(8)], R=[sl, hT], W=[b])
        gh.release()
        yield
        ACT(siluz[j][:, h * 512:(h + 1) * 512], b[:], AF.Silu, [b], [siluz[j]])
        banks.put([b])

    def xbc_chain(gh, c, ch, mode="F"):
        (sl,) = yield from gh.get()
        if mode == "S" and ch >= 10:
            (b,) = yield from banks.get(1)
            P.mm(b[:, 0:3], [(sl[:, kc, c * 128:(c + 1) * 128], hT[:, kc, T - 3:T]) for kc in range(8)], R=[sl, hT], W=[b])
            gh.release()
            yield
            ACT(halo_s[:, ch, :], b[:, 0:3], AF.Copy, [b], [halo_s])
            banks.put([b])
            return
        gw, gd = yield from gens.get(2)
        issue_dg(gd, ch)
        (b,) = yield from banks.get(1)
        fm_mm(b, sl, c * 128, hT)
        gh.release()
        yield
        if ch < 8:
            dst, dt_ = xsT[ch][:], [xsT[ch]]
        elif ch < 10:
            dst, dt_ = BT[ch - 8][:], [BT[ch - 8]]
        else:
            dst, dt_ = CT[ch - 10][:], [CT[ch - 10]]
        bc_ = yield from conv_pe_steps(b, gw, gd, halo_s, ch, 4, ch)
        ACT(dst, bc_[:], AF.Silu, [bc_, cvec], dt_, bias=cvec[:, O_CBS + ch:O_CBS + ch + 1])
        banks.put([bc_])
        gens.put([gw, gd])

    def ly_chain(gh, c, ch, cs):
        n = cs.stop - cs.start
        (sl,) = yield from gh.get()
        (b,) = yield from banks.get(1)
        P.mm(b[:, 0:n], [(sl[:, kc, c * 128:(c + 1) * 128], hT[:, kc, cs]) for kc in range(8)], R=[sl, hT], W=[b])
        gh.release()
        yield
        ACT(gy[ch][:, cs], b[:, 0:n], AF.Gelu_apprx_tanh, [b], [gy[ch]])
        banks.put([b])

    def lx_chain(gh, c, ch, mode="F", cs=slice(0, T)):
        (sl,) = yield from gh.get()
        w, xcf, lr, li, la, lm = yield from gens.get(6)
        issue_dg(lm, 12 + ch)
        (b,) = yield from banks.get(1)
        fm_mm(b, sl, c * 128, hT)
        gh.release()
        yield
        bc_ = yield from conv_pe_steps(b, w, lm, halo_l, ch, 4, 12 + ch)
        xcb = view_bf(w, 516, 516 + T, "xcb")
        ACT(xcf[:, 0:T], bc_[:], AF.Identity, [bc_, cvec], [xcf], bias=cvec[:, O_CBL + ch:O_CBL + ch + 1])
        ACT(xcb[:], bc_[:], AF.Identity, [bc_, cvec], [w], bias=cvec[:, O_CBL + ch:O_CBL + ch + 1])
        banks.put([bc_])
        yield
        br_, bi_ = yield from banks.get(2)
        P.mm(br_[:], [(bd[:, ch, :], xcb[:])], R=[bd, xcb], W=[br_])
        P.mm(bi_[:], [(bd[:, 8 + ch, :], xcb[:])], R=[bd, xcb], W=[bi_])
        yield
        ACT(lr[:, 0:T], br_[:], AF.Sigmoid, [br_, cvec], [lr], bias=cvec[:, O_BR + ch:O_BR + ch + 1])
        ACT(li[:, 0:T], bi_[:], AF.Sigmoid, [bi_, cvec], [li], bias=cvec[:, O_BI + ch:O_BI + ch + 1])
        banks.put([br_, bi_])
        yield
        ACT(la[:, 0:T], lr[:, 0:T], AF.Exp, [lr, cst], [la], scale=cst[:, 16 + ch:17 + ch])
        ACT(lm[:, 0:T], lr[:, 0:T], AF.Exp, [lr, cst], [lm], scale=cst[:, 24 + ch:25 + ch])
        TT(li[:, 0:T], li[:, 0:T], xcf[:, 0:T], ALU.mult, [li, xcf], [li], eng="gpsimd")
        yield
        ACT(lm[:, 0:T], lm[:, 0:T], AF.Ln, [lm], [lm], scale=-1.0, bias=1.0)
        ACT(lm[:, 0:T], lm[:, 0:T], AF.Exp, [lm], [lm], scale=0.5)
        yield
        TT(li[:, 0:T], li[:, 0:T], lm[:, 0:T], ALU.mult, [li, lm], [li])
        yield
        op("vector", lambda e: e.tensor_tensor_scan(out=lr[:, 0:T], data0=la[:, 0:T], data1=li[:, 0:T], initial=lru_h[:, ch:ch + 1],
                                                    op0=ALU.mult, op1=ALU.add),
           [la, li, lru_h], [lr])
        yield
        CP(lru_h[:, ch:ch + 1], lr[:, T - 1:T], [lr], [lru_h], eng="gpsimd")
        if mode != "S":
            TT(y_bT[:, ch, cs], lr[:, cs], gy[ch][:, cs], ALU.mult, [lr, gy[ch]], [y_bT])
        gens.put([w, xcf, lr, li, la, lm])

    def ssd_chain(ti, j, mode="F", tmode="F"):
        if True:
            (st,) = yield from ssds.get(1)
            dtb, exb, MT, cbTm, xs_dec, ybuf, ya_tok, xsD, xdt = (st[k] for k in ("dtb", "exb", "MT", "cbTm", "xs_dec", "ybuf", "ya_tok", "xsD", "xdt"))
            ue4 = [st["ue4"]] * 2
            exps = [st["exps"]] * 2
            js = slice(j * 128, (j + 1) * 128)

            def bump():
                P.defer(lambda: state_ver.__setitem__(ti, state_ver.get(ti, 0) + 1))
            (bdt,) = yield from banks.get(1)
            P.mm(bdt[:, 0:16], [(hT[:, kc, js], wdt[:, kc, :]) for kc in range(8)], R=[hT, wdt], W=[bdt])
            TT(dtb[:, 0:16], bdt[:, 0:16], rowb[:, R_DTB:R_DTB + 16], ALU.add, [bdt, rowb], [dtb])
            banks.put([bdt])
            yield
            ACT(dtb[:, 0:16], dtb[:, 0:16], AF.Exp, [dtb], [dtb])
            ACT(dtb[:, 0:16], dtb[:, 0:16], AF.Ln, [dtb], [dtb], bias=1.0)
            TT(dtb[:, 16:32], dtb[:, 0:16], AROW, ALU.mult, [dtb, cst], [dtb])
            yield
            (bcs,) = yield from banks.get(1)
            P.mm(bcs[:, 0:16], [(U, dtb[:, 16:32])], R=[tri, dtb], W=[bcs], signal=False)
            P.mm(bcs[:, 16:32], [(G, dtb[:, 16:32])], R=[tri, dtb], W=[bcs], signal=False)
            P.mm(bcs[:, 32:48], [(ONES, dtb[:, 16:32])], R=[tri, dtb], W=[bcs])
            ACT(exb[:], bcs[:, 0:48], AF.Exp, [bcs], [exb])
            banks.put([bcs])
            yield
            TT(dtb[:, 32:48], dtb[:, 0:16], exb[:, 16:32], ALU.mult, [dtb, exb], [dtb])
            TT(xs_dec[:, :].rearrange("p (e d) -> p e d", d=64), xs_tok[j][:, :].rearrange("p (e d) -> p e d", d=64),
               bc(dtb[:, 32:48], [128, 16, 64]), ALU.mult, [xs_tok[j], dtb], [xs_dec])
            if mode != "S":
                TT(xsD[:, :].rearrange("p (e d) -> p e d", d=64), xs_tok[j][:, :].rearrange("p (e d) -> p e d", d=64),
                   bc(rowb[:, R_D:R_D + 16], [128, 16, 64]), ALU.mult, [xs_tok[j], rowb], [xsD], eng="gpsimd")
                TT(xdt[:, :].rearrange("p (e d) -> p e d", d=64), xs_tok[j][:, :].rearrange("p (e d) -> p e d", d=64),
                   bc(dtb[:, 0:16], [128, 16, 64]), ALU.mult, [xs_tok[j], dtb], [xdt], eng="gpsimd")
            yield
            if mode == "S":
                while state_ver.get(ti, 0) < j:
                    yield "blocked"
                bst = yield from banks.get(2)
                for g in range(2):
                    P.mm(bst[g][:], [(B_tok[j][:, g * 128:(g + 1) * 128], xs_dec[:, g * 512:(g + 1) * 512])],
                         R=[B_tok[j], xs_dec], W=[bst[g]])
                TT(ssm_state[:, :].rearrange("p (e d) -> p e d", d=64), ssm_state[:, :].rearrange("p (e d) -> p e d", d=64),
                   bc(exb[:, 32:48], [128, 16, 64]), ALU.mult, [ssm_state, exb], [ssm_state])
                for g in range(2):
                    gs = slice(g * 512, (g + 1) * 512)
                    TT(ssm_state[:, gs], ssm_state[:, gs], bst[g][:], ALU.add, [ssm_state, bst[g]], [ssm_state])
                banks.put(bst)
                if j == NJ - 1 or tmode == "H":
                    ACT(ssm_state_bf[:], ssm_state[:], AF.Copy, [ssm_state], [ssm_state_bf])
                bump()
                ssds.put([st])
                return
            (bcb,) = yield from banks.get(1)
            for g in range(2):
                P.mm(bcb[:, g * 128:(g + 1) * 128], [(BT[g][:, js], CT[g][:, js])], R=[BT[g], CT[g]], W=[bcb], signal=(g == 1))
            TT(cbTm[:, :, :], bcb[:, 0:256].rearrange("p (g l) -> p g l", l=128), U.unsqueeze(1).broadcast_to([128, 2, 128]),
               ALU.mult, [bcb, tri], [cbTm])
            banks.put([bcb])
            yield
            for q in range(4):
                u4 = ue4[q % 2]
                ex4 = exps[q % 2]
                TT(u4[:, :, :], U.unsqueeze(1).broadcast_to([128, 4, 128]), bc(dtb[:, 16 + q * 4:20 + q * 4], [128, 4, 128]),
                   ALU.mult, [tri, dtb], [u4], eng="gpsimd")
                yield
                (bsg,) = yield from banks.get(1)
                for e4 in range(4):
                    P.mm(bsg[:, e4 * 128:(e4 + 1) * 128], [(G, u4[:, e4, :])], R=[tri, u4], W=[bsg], signal=(e4 == 3))
                ACT(ex4[:, :, :], bsg[:, :].rearrange("p (e l) -> p e l", l=128), AF.Exp, [bsg], [ex4])
                banks.put([bsg])
                yield
                TT(MT[:, q * 4:(q + 1) * 4, :], ex4[:, :, :], cbTm[:, q // 2, :].unsqueeze(1).broadcast_to([128, 4, 128]), ALU.mult,
                   [ex4, cbTm], [MT])
                yield
            while state_ver.get(ti, 0) < j:
                yield "blocked"
            byo = yield from banks.get(2)
            for g in range(2):
                P.mm(byo[g][:], [(CT[g][:, js], ssm_state_bf[:, g * 512:(g + 1) * 512])], R=[CT[g], ssm_state_bf], W=[byo[g]])
            for g in range(2):
                gs = slice(g * 512, (g + 1) * 512)
                TT(ybuf[:, gs].rearrange("p (e d) -> p e d", d=64), byo[g][:, :].rearrange("p (e d) -> p e d", d=64),
                   bc(exb[:, g * 8:(g + 1) * 8], [128, 8, 64]), ALU.mult, [byo[g], exb], [ybuf])
            banks.put(byo)
            yield
            byd = yield from banks.get(2)
            for g in range(2):
                def fn(e, g=g, j=j, byd=byd):
                    e.matmul(byd[g][:], ident[:], xsD[:, g * 512:(g + 1) * 512], start=True, stop=False)
                    inst = None
                    for e8 in range(8):
                        e_ = g * 8 + e8
                        inst = e.matmul(byd[g][:, e8 * 64:(e8 + 1) * 64], MT[:, e_, :], xdt[:, e_ * 64:(e_ + 1) * 64],
                                        start=False, stop=(e8 == 7))
                    return inst
                op("tensor", fn, [MT, xdt, xsD, ident], [byd[g]], dur=0.3 + 8 * 0.1)
            for g in range(2):
                gs = slice(g * 512, (g + 1) * 512)
                TT(ybuf[:, gs], ybuf[:, gs], byd[g][:], ALU.add, [ybuf, byd[g]], [ybuf])
            banks.put(byd)
            yield
            bst = yield from banks.get(2)
            for g in range(2):
                P.mm(bst[g][:], [(B_tok[j][:, g * 128:(g + 1) * 128], xs_dec[:, g * 512:(g + 1) * 512])],
                     R=[B_tok[j], xs_dec], W=[bst[g]])
            TT(ssm_state[:, :].rearrange("p (e d) -> p e d", d=64), ssm_state[:, :].rearrange("p (e d) -> p e d", d=64),
               bc(exb[:, 32:48], [128, 16, 64]), ALU.mult, [ssm_state, exb], [ssm_state])
            for g in range(2):
                gs = slice(g * 512, (g + 1) * 512)
                TT(ssm_state[:, gs], ssm_state[:, gs], bst[g][:], ALU.add, [ssm_state, bst[g]], [ssm_state])
            banks.put(bst)
            ACT(ssm_state_bf[:], ssm_state[:], AF.Copy, [ssm_state], [ssm_state_bf])
            bump()
            yield
            if ti == 0:
                dump(f"y{j}", ybuf[:], [128, 1024], F32, [ybuf])
            sg = stat_g[j]
            TT(ybuf[:], ybuf[:], siluz[j][:], ALU.mult, [ybuf, siluz[j]], [ybuf])
            for g in range(2):
                ACT(junk[:, 0:512], ybuf[:, g * 512:(g + 1) * 512], AF.Square, [ybuf], [junk, sg], accum_out=sg[:, g:g + 1])
            rstd_from_ss(sg[:, 2:4], sg[:, 0:2], 512, sg)
            yield
            for g in range(2):
                gs = slice(g * 512, (g + 1) * 512)
                STT(ya_tok[:, gs], ybuf[:, gs], sg[:, 2 + g:3 + g], rowb[:, R_SN + g * 512:R_SN + (g + 1) * 512], ALU.mult, ALU.mult,
                    [ybuf, sg, rowb], [ya_tok])
            yield
            for kc in range(8):
                op("tensor", lambda e, kc=kc: e.transpose(ptr[:, kc * 128:(kc + 1) * 128], ya_tok[:, kc * 128:(kc + 1) * 128], ident[:]),
                   [ya_tok, ident], [ptr], signal=(kc == 7))
            CP(y_aT[:, :, js], ptr[:, :].rearrange("p (c t) -> p c t", t=128), [ptr], [y_aT])
            ssds.put([st])

    def merge_chain(gh, c, m, cs):
        n = cs.stop - cs.start
        s_p, s_g = yield from gh.get()
        ga, gb, gt, gt2 = yield from gens.get(4)
        b3, b4 = yield from banks.get(2)

        def mmc(b, sl, co, rhsT):
            P.mm(b[:, 0:n], [(sl[:, kc, co:co + 128], rhsT[:, kc, cs]) for kc in range(8)], R=[sl, rhsT], W=[b])
        mmc(b3, s_g, c * 128, hT)
        mmc(b4, s_g, 256 + c * 128, hT)
        yield
        ACT(ga[:, 0:n], b3[:, 0:n], AF.Sigmoid, [b3, cvec], [ga], bias=cvec[:, O_GB + m:O_GB + m + 1])
        ACT(gb[:, 0:n], b4[:, 0:n], AF.Sigmoid, [b4, cvec], [gb], bias=cvec[:, O_GB + 8 + m:O_GB + 9 + m])
        banks.put([b3, b4])
        yield
        b1, b2 = yield from banks.get(2)
        mmc(b1, s_p, c * 128, y_aT)
        mmc(b2, s_p, 256 + c * 128, y_bT)
        gh.release()
        yield
        TT(gt[:, 0:n], ga[:, 0:n], b1[:, 0:n], ALU.mult, [ga, b1], [gt])
        TT(gt2[:, 0:n], gb[:, 0:n], b2[:, 0:n], ALU.mult, [gb, b2], [gt2])
        banks.put([b1, b2])
        yield
        TT(mergedT[:, m, cs], gt[:, 0:n], gt2[:, 0:n], ALU.add, [gt, gt2], [mergedT])
        gens.put([ga, gb, gt, gt2])

    def wout_chain(gh, j):
        js = slice(j * 128, (j + 1) * 128)
        so = stat_o[j]
        s_wo = yield from gh.get()
        yb, yb2 = yield from gens.get(2)
        bo = yield from banks.get(2)
        for h in range(2):
            P.mm(bo[h][:], [(mergedT[:, kc, js], s_wo[h][:, kc, :]) for kc in range(8)], R=[mergedT, s_wo[h]], W=[bo[h]])
        gh.release()
        yield
        for h in range(2):
            ACT(junk[:, 0:512], bo[h][:], AF.Square, [bo[h]], [junk, so], accum_out=so[:, h:h + 1])
        TT(so[:, 2:3], so[:, 0:1], so[:, 1:2], ALU.add, [so], [so])
        rstd_from_ss(so[:, 3:4], so[:, 2:3], D, so)
        yield
        ybs = [yb, yb2]
        for h in range(2):
            hs = slice(h * 512, (h + 1) * 512)
            STT(ybs[h][:, 0:T], bo[h][:], so[:, 3:4], rowb[:, R_PN1 + h * 512:R_PN1 + (h + 1) * 512], ALU.mult, ALU.mult,
                [bo[h], so, rowb], [ybs[h]])
        banks.put(bo)
        yield
        for h in range(2):
            hs = slice(h * 512, (h + 1) * 512)
            TT(x_tok[j][:, hs], x_tok[j][:, hs], ybs[h][:, 0:T], ALU.add, [x_tok[j], ybs[h]], [x_tok[j]])
        gens.put(ybs)

    def ffn_chain(gh, c, ch):
        s_g, s_v = yield from gh.get()
        wg, ag, wv, av = yield from gens.get(4)
        bg_, bv_ = yield from banks.get(2)
        fm_mm(bg_, s_g, c * 128, hT)
        fm_mm(bv_, s_v, c * 128, hT)
        gh.release()
        yield
        g1 = conv_steps(bg_, wg, ag, halo_f, ch, 3, O_CWF + ch * 3, O_CBF + ch, AF.Gelu_apprx_tanh)
        g2 = conv_steps(bv_, wv, av, halo_f, 24 + ch, 3, O_CWF + (24 + ch) * 3, O_CBF + 24 + ch, None)
        alive = [g1, g2]
        while alive:
            for g in list(alive):
                try:
                    next(g)
                except StopIteration:
                    alive.remove(g)
            yield
        TT(fT[:, ch, :], ag[:, 0:T], av[:, 0:T], ALU.mult, [ag, av], [fT], eng="gpsimd")
        gens.put([wg, ag, wv, av])

    def ffn_halo_chain(gh, c, ch):
        s_g, s_v = yield from gh.get()
        (b,) = yield from banks.get(1)
        P.mm(b[:, 0:2], [(s_g[:, kc, c * 128:(c + 1) * 128], hT[:, kc, T - 2:T]) for kc in range(8)], R=[s_g, hT], W=[b], signal=False)
        P.mm(b[:, 2:4], [(s_v[:, kc, c * 128:(c + 1) * 128], hT[:, kc, T - 2:T]) for kc in range(8)], R=[s_v, hT], W=[b])
        gh.release()
        yield
        ACT(halo_f[:, ch, :], b[:, 0:2], AF.Copy, [b], [halo_f])
        ACT(halo_f[:, 24 + ch, :], b[:, 2:4], AF.Copy, [b], [halo_f])
        banks.put([b])

    retire([bdf], ssd_sets[0]["alias"] + ssd_sets[1]["alias"])
    stg = seq(gens.get(2))
    for ci in range(20):
        g_ = stg[ci % 2]
        woff = (O_CWS + ci * 4) if ci < 12 else (O_CWL + (ci - 12) * 4)
        gv = g_[:, :].bitcast(BF16)
        for k in range(4):
            TS(gv[:, k * 128:(k + 1) * 128], ident[:], cvec[:, woff + k:woff + k + 1], None, ALU.mult, None, [ident, cvec], [g_])
        P.dma("sync", dgscr[ci], gv[:, 0:512], f"d_dgs{ci % 2}", R=[g_], W=[dgbuf[ci]])
    gens.put(stg)
    n_out = 0
    for ti in range(NT):
        t0 = ti * T
        mode = modes[ti]
        if mode == "F" and ti > 0 and modes[ti - 1] != "F":
            fl = flag[:, 0:1]
            TS(ssm_state[:], ssm_state[:], fl, None, ALU.mult, None, [ssm_state, flag], [ssm_state])
            ACT(ssm_state_bf[:], ssm_state[:], AF.Copy, [ssm_state], [ssm_state_bf])
            TS(lru_h[:], lru_h[:], fl, None, ALU.mult, None, [lru_h, flag], [lru_h])
            for hl in (halo_s, halo_l, halo_f):
                TS(hl[:], hl[:], fl, None, ALU.mult, None, [hl, flag], [hl])
        retire([fT], siluz + xsT + xs_tok)
        retire(d_tok, gy + [y_bT])
        for j in range(NJ):
            P.dma("sync", x_tok[j][:], dram["x"][t0 + j * 128:t0 + (j + 1) * 128, :], f"d_x{j}", W=[x_tok[j]])
        if mode != "S":
            issue_conversions(len(conv_todo))
        norm_and_transpose(R_W1)

        def p2_chains():
            jz = range(NJ) if mode == "F" else [NJ - 1]
            for h in range(0 if mode == "S" else 2):
                gh = GH(1, len(jz))
                for j in jz:
                    yield z_chain(gh, h, j)
            for h in range(3):
                gh = GH(1, 4)
                for c in range(4):
                    yield xbc_chain(gh, c, h * 4 + c, mode)
        run(p2_chains(), 3)
        for j in range(NJ):
            for c in range(8):
                op("tensor", lambda e, c=c, j=j: e.transpose(ptr[:, c * 128:(c + 1) * 128], xsT[c][:, j * 128:(j + 1) * 128], ident[:]),
                   [xsT[c], ident], [ptr], signal=(c == 7))
            CP(xs_tok[j][:], ptr[:, :], [ptr], [xs_tok[j]])
            for g in range(2):
                op("tensor", lambda e, g=g, j=j: e.transpose(ptr[:, g * 128:(g + 1) * 128], BT[g][:, j * 128:(j + 1) * 128], ident[:]),
                   [BT[g], ident], [ptr], signal=(g == 1))
            CP(B_tok[j][:], ptr[:, 0:256], [ptr], [B_tok[j]])
        if ti == 0 and mode == "F":
            dump("hT", hT[:], [128, 8, T], BF16, [hT])
            dump("siluz0", siluz[0][:], [128, 1024], BF16, [siluz[0]])
            dump("xs_tok0", xs_tok[0][:], [128, 1024], BF16, [xs_tok[0]])
        retire(xsT, [mergedT])

        cs = slice(0, T) if mode == "F" else slice(T - 128, T)

        def p3_chains():
            for h in range(0 if mode == "S" else 2):
                gh = GH(1, 4)
                for c in range(4):
                    yield ly_chain(gh, c, h * 4 + c, cs)
            for h in range(2):
                gh = GH(1, 4)
                for c in range(4):
                    yield lx_chain(gh, c, h * 4 + c, mode, cs)
        smode = lambda j: mode if (mode != "H" or j == NJ - 1) else "S"
        run(p3_chains(), 2, extra=[ssd_chain(ti, j, smode(j), mode) for j in range(NJ)], extra_steps=1)
        if mode == "S":
            issue_conversions(conv_per_tile)
            continue
        if ti == 0:
            dump("y_aT", y_aT[:], [128, 8, T], BF16, [y_aT])
            dump("y_bT", y_bT[:], [128, 8, T], BF16, [y_bT])

        def p4_chains():
            for q in range(4):
                gh = GH(2, 2)
                for c in range(2):
                    yield merge_chain(gh, c, q * 2 + c, cs)
        run(p4_chains(), 3)
        jl_o = range(NJ) if mode == "F" else [NJ - 1]
        gh_wo = GH(2, len(jl_o))
        run((wout_chain(gh_wo, j) for j in jl_o), 2)
        if ti == 0:
            dump("mergedT", mergedT[:], [128, 8, T], BF16, [mergedT])
            for j in range(NJ):
                dump(f"x1_{j}", x_tok[j][:], [128, 1024], F32, [x_tok[j]])

        retire(siluz + [mergedT] + xs_tok, [fT])
        retire(gy + [y_bT], d_tok)
        if mode == "H":
            norm_and_transpose(R_W2, [NJ - 1])

            def p5h_chains():
                for q in range(6):
                    gh = GH(2, 4)
                    for c in range(4):
                        yield ffn_halo_chain(gh, c, q * 4 + c)
            run(p5h_chains(), 3)
            continue
        norm_and_transpose(R_W2)

        def p5_chains():
            for q in range(6):
                gh = GH(2, 4)
                for c in range(4):
                    yield ffn_chain(gh, c, q * 4 + c)
        run(p5_chains(), 3)
        if ti == 0:
            dump("fT", fT[:], [128, 24, T], BF16, [fT])
        for h in range(2):
            bd_ = seq(banks.get(NJ))
            for kb in range(3):
                ghd = GH(1, 1)
                (sl,) = seq(ghd.get())
                for j in range(NJ):
                    js = slice(j * 128, (j + 1) * 128)

                    def fn(e, j=j, kb=kb, sl=sl, js=js, bd_=bd_):
                        inst = None
                        for k8 in range(8):
                            inst = e.matmul(bd_[j][:], fT[:, kb * 8 + k8, js], sl[:, k8, :], start=(kb == 0 and k8 == 0),
                                            stop=(kb == 2 and k8 == 7))
                        return inst
                    op("tensor", fn, [fT, sl], [bd_[j]], signal=True)
                ghd.release()
            for j in range(NJ):
                hs = slice(h * 512, (h + 1) * 512)
                ACT(d_tok[j][:, hs], bd_[j][:], AF.Copy, [bd_[j]], [d_tok[j]])
                ACT(junk[:, 0:512], d_tok[j][:, hs], AF.Square, [d_tok[j]], [junk, stat_f], accum_out=stat_f[:, h * 4 + j:h * 4 + j + 1])
            banks.put(bd_)
        TT(stat_f[:, 8:12], stat_f[:, 0:4], stat_f[:, 4:8], ALU.add, [stat_f], [stat_f])
        rstd_from_ss(stat_f[:, 12:16], stat_f[:, 8:12], D, stat_f)
        for j in range(NJ):
            STT(d_tok[j][:], d_tok[j][:], stat_f[:, 12 + j:13 + j], rowb[:, R_PN2:R_PN2 + D], ALU.mult, ALU.mult,
                [d_tok[j], stat_f, rowb], [d_tok[j]])
            TT(x_tok[j][:], x_tok[j][:], d_tok[j][:], ALU.add, [x_tok[j], d_tok[j]], [x_tok[j]])
            P.dma("sync", d_out[n_out * T + j * 128:n_out * T + (j + 1) * 128, :], x_tok[j][:], f"d_o{j}", R=[x_tok[j]])
        n_out += 1

    P.wait_all("sync", [(f"d_o{j}", P.cnt[f"d_o{j}"]) for j in range(NJ)] + dbg_toks)
    P.emit()
    return nc


def prep_shared(inp):
    f = np.float32
    g = lambda k: np.asarray(inp[k], dtype=f)[0]
    sh = {}
    sh["w_in"] = np.ascontiguousarray(g("w_in"))
    sh["p_a"] = np.ascontiguousarray(g("w_proj_ssm"))
    sh["p_b"] = np.ascontiguousarray(g("w_proj_lru"))
    sh["w_out"] = np.ascontiguousarray(g("w_out"))
    sh["w_up"] = np.ascontiguousarray(g("w_ffn_up"))
    sh["w_down"] = np.ascontiguousarray(g("w_ffn_down"))
    bd = np.zeros((128, 16, 128), f)
    for gi, k in enumerate(("lru_wr", "lru_wi")):
        w = g(k)
        for ch in range(8):
            for hl in range(2):
                bd[hl * 64:(hl + 1) * 64, gi * 8 + ch, hl * 64:(hl + 1) * 64] = w[ch * 2 + hl]
    sh["bd"] = bd

    def cols(v):
        return np.ascontiguousarray(v.reshape(-1, 128).T)
    cv = np.zeros((128, NCV), f)
    cws = g("ssm_conv_w")
    for k in range(4):
        cv[:, O_CWS + k:O_CWS + 48:4] = cols(cws[k])
    cv[:, O_CBS:O_CBS + 12] = cols(g("ssm_conv_b"))
    cwl = g("lru_conv_w")
    for k in range(4):
        cv[:, O_CWL + k:O_CWL + 32:4] = cols(cwl[k])
    cv[:, O_CBL:O_CBL + 8] = cols(g("lru_conv_b"))
    cv[:, O_BR:O_BR + 8] = cols(g("lru_br").reshape(-1))
    cv[:, O_BI:O_BI + 8] = cols(g("lru_bi").reshape(-1))
    cv[:, O_LAM:O_LAM + 8] = cols(g("lru_lambda"))
    gbv = g("gate_b")
    cv[:, O_GB:O_GB + 8] = cols(gbv[0])
    cv[:, O_GB + 8:O_GB + 16] = cols(gbv[1])
    cwf = g("ffn_conv_w")
    for k in range(3):
        cv[:, O_CWF + k:O_CWF + 144:3] = cols(cwf[k])
    cv[:, O_CBF:O_CBF + 48] = cols(g("ffn_conv_b"))
    sh["cvec"] = cv
    rb = np.zeros((128, NRB), f)
    rb[:, R_PN1:R_PN1 + D] = g("mix_post_norm")[None]
    rb[:, R_PN2:R_PN2 + D] = g("ffn_post_norm")[None]
    rb[:, R_SN:R_SN + D] = g("ssm_norm")[None]
    rb[:, R_W1:R_W1 + D] = g("mix_pre_norm")[None]
    rb[:, R_W2:R_W2 + D] = g("ffn_pre_norm")[None]
    rb[:, R_DTB:R_DTB + 16] = g("ssm_dt_bias")[None]
    rb[:, R_ALOG:R_ALOG + 16] = g("ssm_a_log")[None]
    rb[:, R_D:R_D + 16] = g("ssm_d")[None]
    sh["rowb"] = rb
    sh["ident"] = np.eye(128).astype(ml_dtypes.bfloat16)
    k = np.arange(128)
    tri = np.zeros((128, 3, 128), f)
    tri[:, 0, :] = (k[:, None] <= k[None, :])
    tri[:, 1, :] = (k[:, None] > k[None, :])
    tri[:, 2, :] = 1.0
    sh["tri"] = tri
    return sh


PREFIX_MODES = ["S", "S", "S", "H"]


def kernel(**inputs):
    x = np.asarray(inputs["x"], dtype=np.float32)
    B, S, _ = x.shape
    sh = prep_shared(inputs)
    half = S // 2
    n_own = half // T
    modes = PREFIX_MODES[-(half // T):] + ["F"] * n_own
    nc = build_program(modes)
    in_maps = []
    for c in range(8):
        b, hf = c // 2, c % 2
        m = dict(sh)
        pre = x[b, :half] if hf else np.zeros((half, D), np.float32)
        m["x"] = np.ascontiguousarray(np.concatenate([pre, x[b, hf * half:(hf + 1) * half]], axis=0))
        m["flag"] = np.full((128, 1), float(hf), np.float32)
        in_maps.append(m)
    res = run_bass_kernel_spmd(nc, in_maps, core_ids=list(range(8)))
    out = np.empty((B, S, D), np.float32)
    for c in range(8):
        b, hf = c // 2, c % 2
        out[b, hf * half:(hf + 1) * half] = res.results[c]["out"]
    return out
```

```python
import contextlib
import numpy as np
import ml_dtypes
import concourse.bass as bass
import concourse.mybir as mybir
from concourse.bass_utils import run_bass_kernel_spmd

F32 = mybir.dt.float32
BF16 = mybir.dt.bfloat16
AF = mybir.ActivationFunctionType
ALU = mybir.AluOpType
AX = mybir.AxisListType

ENGS = ("sync", "scalar", "vector", "gpsimd", "tensor")


class Buf:
    __slots__ = ("name", "w", "r", "tw", "tr")

    def __init__(self, name):
        self.name = name
        self.w = None
        self.r = []
        self.tw = 0.0
        self.tr = 0.0


class Tile:
    def __init__(self, t, name):
        self.t = t
        self.buf = Buf(name)

    def __getitem__(self, k):
        return self.t[k]


class Prog:
    def __init__(self, nc):
        self.nc = nc
        self.es = contextlib.ExitStack()
        self.q = {e: [] for e in ENGS}
        self.sem = {}
        self.cnt = {}
        self.seen = {e: {} for e in ENGS}
        for e in ENGS:
            self.newsem("E_" + e)
        self.n_t = 0
        self.fifo = None
        self.act_set = None
        self.eng_free = {e: 0.0 for e in ENGS}

    def newsem(self, name):
        if name not in self.sem:
            self.sem[name] = self.es.enter_context(self.nc.semaphore(name))
            self.cnt[name] = 0
        return name

    def sb(self, shape, dtype, name=None):
        self.n_t += 1
        name = "s_" + (name or f"t{self.n_t}")
        t = self.es.enter_context(self.nc.sbuf_tensor(name, list(shape), dtype))
        return Tile(t, name)

    def ps(self, shape, dtype, name=None):
        self.n_t += 1
        name = "ps_" + (name or f"p{self.n_t}")
        t = self.es.enter_context(self.nc.psum_tensor(name, list(shape), dtype))
        return Tile(t, name)

    def _deps(self, eng, R, W):
        need = {}
        for b in R:
            b = b.buf if isinstance(b, Tile) else b
            if b.w is not None:
                s, v = b.w
                need[s] = max(need.get(s, 0), v)
        for b in W:
            b = b.buf if isinstance(b, Tile) else b
            if b.w is not None:
                s, v = b.w
                need[s] = max(need.get(s, 0), v)
            for s, v in b.r:
                need[s] = max(need.get(s, 0), v)
        waits = []
        seen = self.seen[eng]
        for s, v in need.items():
            if eng == "tensor" and s == "E_tensor":
                continue
            if seen.get(s, 0) < v:
                seen[s] = v
                waits.append((s, v))
        return waits

    def _mark(self, tok, R, W):
        for b in R:
            b = b.buf if isinstance(b, Tile) else b
            b.r.append(tok)
            if len(b.r) > 24:
                m = {}
                for s, v in b.r:
                    m[s] = max(m.get(s, 0), v)
                b.r = list(m.items())
        for b in W:
            b = b.buf if isinstance(b, Tile) else b
            b.w = tok
            b.r = []

    @staticmethod
    def _bufs(xs):
        return tuple(b.buf if isinstance(b, Tile) else b for b in xs)

    def op(self, eng, fn, R=(), W=(), signal=True, dur=0.5, aset=None):
        d = ("op", eng, fn, self._bufs(R), self._bufs(W), signal, dur, aset)
        if self.fifo is not None:
            self.fifo.append(d)
        else:
            self._emit(d)

    def dma(self, queue, out, in_, sem, R=(), W=(), dur=8.0):
        self.newsem(sem)
        d = ("dma", queue, (out, in_, sem), self._bufs(R), self._bufs(W), True, dur, None)
        if self.fifo is not None:
            self.fifo.append(d)
            return (sem, self.cnt[sem] + 16)
        return self._emit(d)

    def defer(self, cb):
        if self.fifo is not None:
            self.fifo.append(("cb", cb))
        else:
            cb()

    def est_start(self, d):
        _, eng, _, R, W, _, _, aset = d
        t = self.eng_free[eng]
        for b in R:
            t = max(t, b.tw)
        for b in W:
            t = max(t, b.tw, b.tr)
        if aset is not None and aset != self.act_set:
            t += 2.5
        return t

    def _emit(self, d):
        kind, eng, fn, R, W, signal, dur, aset = d
        start = self.est_start(d)
        if aset is not None:
            self.act_set = aset
        waits = self._deps(eng, R, W)
        if kind == "dma":
            out, in_, sem = fn
            self.cnt[sem] += 16
            tok = (sem, self.cnt[sem])
            self.q[eng].append((waits, lambda e: e.dma_start(out=out, in_=in_), (sem, 16)))
            self.eng_free[eng] = start + 1.0
            fin = start + dur
        else:
            sname = "E_" + eng
            if signal:
                self.cnt[sname] += 1
                tok = (sname, self.cnt[sname])
            else:
                tok = (sname, self.cnt[sname] + 1)
            self.q[eng].append((waits, fn, (sname, 1) if signal else None))
            fin = start + dur
            self.eng_free[eng] = fin
        fin_vis = fin + 0.15
        for b in R:
            b.tr = max(b.tr, fin_vis)
        for b in W:
            b.tw = fin_vis
            b.tr = 0.0
        self._mark(tok, R, W)
        return tok

    def mm(self, out, pairs, R=(), W=(), signal=True):
        n = len(pairs)

        def fn(e):
            inst = None
            for i, (l, r) in enumerate(pairs):
                inst = e.matmul(out, l, r, start=(i == 0), stop=(i == n - 1))
            return inst
        try:
            nfree = pairs[0][1].free_size()
        except Exception:
            nfree = 512
        return self.op("tensor", fn, R, W, signal, dur=n * (0.07 + nfree / 1900.0))

    def wait_all(self, eng, toks):
        waits = []
        for s, v in toks:
            if self.seen[eng].get(s, 0) < v:
                self.seen[eng][s] = v
                waits.append((s, v))
        self.q[eng].append((waits, None, None))

    def emit(self):
        nc = self.nc
        with nc.Block() as block:
            def make(eng_name):
                def body(eng):
                    for waits, fn, inc in self.q[eng_name]:
                        for s, v in waits:
                            eng.wait_ge(self.sem[s], v)
                        if fn is not None:
                            inst = fn(eng)
                            if inc is not None:
                                inst.then_inc(self.sem[inc[0]], inc[1])
                return body
            for e in ENGS:
                if self.q[e]:
                    getattr(block, e)(make(e))
        self.es.close()


D = 1024
NIN = 6672
FF = 3072
T = 512
NJ = T // 128
EPS = 1e-6

O_CWS, O_CBS, O_CWL, O_CBL, O_BR, O_BI, O_LAM, O_GB, O_CWF, O_CBF, NCV = 0, 48, 60, 92, 100, 108, 116, 124, 140, 284, 332
R_PN1, R_PN2, R_SN, R_W1, R_W2, R_DTB, R_ALOG, R_D, NRB = 0, 1024, 2048, 3072, 4096, 5120, 5136, 5152, 5168

def _groups(mode="F"):
    g = []
    if mode != "S":
        for h in range(2):
            g.append(("w_in", 0, h * 512))
    for h in range(3):
        g.append(("w_in", 0, 1024 + h * 512))
    if mode != "S":
        for h in range(2):
            g.append(("w_in", 0, 2576 + h * 512))
    for h in range(2):
        g.append(("w_in", 0, 3600 + h * 512))
    if mode == "S":
        return g
    for q in range(4):
        g.append([("p_a", 0, q * 256, 256, 0), ("p_b", 0, q * 256, 256, 256)])
        g.append([("w_in", 0, 4624 + q * 256, 256, 0), ("w_in", 0, 4624 + 1024 + q * 256, 256, 256)])
    for h in range(2):
        g.append(("w_out", 0, h * 512))
    for q in range(6):
        g.append(("w_up", 0, q * 512))
        g.append(("w_up", 0, 3072 + q * 512))
    if mode == "H":
        return g
    for h in range(2):
        for kb in range(3):
            g.append(("w_down", kb, h * 512))
    return g


NSLOT = 3


def build_program(modes, dbg=False):
    nc = bass.Bass("TRN2", target_bir_lowering=False)
    if isinstance(modes, int):
        modes = ["F"] * modes
    NT = len(modes)
    S = NT * T
    NF = sum(1 for m in modes if m == "F")
    ALLG = _groups("F")
    key = lambda g: repr(g)
    gid_of = {key(g): i for i, g in enumerate(ALLG)}
    GROUPS = []
    for m in modes:
        GROUPS += [gid_of[key(g)] for g in _groups(m)]
    dram = {}
    dram["x"] = nc.dram_tensor("x", [S, D], F32, kind="ExternalInput").ap()
    dram["w_in"] = nc.dram_tensor("w_in", [D, NIN], F32, kind="ExternalInput").ap()
    dram["p_a"] = nc.dram_tensor("p_a", [D, D], F32, kind="ExternalInput").ap()
    dram["p_b"] = nc.dram_tensor("p_b", [D, D], F32, kind="ExternalInput").ap()
    dram["w_out"] = nc.dram_tensor("w_out", [D, D], F32, kind="ExternalInput").ap()
    dram["w_up"] = nc.dram_tensor("w_up", [D, 2 * FF], F32, kind="ExternalInput").ap()
    dram["w_down"] = nc.dram_tensor("w_down", [FF, D], F32, kind="ExternalInput").ap()
    d_bd = nc.dram_tensor("bd", [128, 16, 128], F32, kind="ExternalInput").ap()
    d_cvec = nc.dram_tensor("cvec", [128, NCV], F32, kind="ExternalInput").ap()
    d_rowb = nc.dram_tensor("rowb", [128, NRB], F32, kind="ExternalInput").ap()
    d_ident = nc.dram_tensor("ident", [128, 128], BF16, kind="ExternalInput").ap()
    d_tri = nc.dram_tensor("tri", [128, 3, 128], F32, kind="ExternalInput").ap()
    d_out = nc.dram_tensor("out", [NF * T, D], F32, kind="ExternalOutput").ap()
    d_flag = nc.dram_tensor("flag", [128, 1], F32, kind="ExternalInput").ap()

    P = Prog(nc)
    op = P.op
    dbg_toks = []

    def dump(name, ap, shape, dtype, R):
        if not dbg:
            return
        d = nc.dram_tensor("dbg_" + name, list(shape), dtype, kind="ExternalOutput").ap()
        dbg_toks.append(P.dma("sync", d, ap, "d_dbg_" + name, R=R))

    def vdur(out, eng="vector"):
        n = out.free_size()
        return (0.12 + n / 960.0) if eng == "vector" else (0.25 + n / 450.0)

    ASET = {AF.Exp: "ln_exp", AF.Ln: "ln_exp", AF.Sigmoid: "sig", AF.Silu: "silu", AF.Gelu_apprx_tanh: "gelu"}

    def ACT(out, in_, func, R, W, **kw):
        return op("scalar", lambda e: e.activation(out=out, in_=in_, func=func, **kw), R, W, dur=0.22 + out.free_size() / 1200.0,
                  aset=ASET.get(func))

    def TT(out, a, b, alu, R, W, eng="vector"):
        return op(eng, lambda e: e.tensor_tensor(out=out, in0=a, in1=b, op=alu), R, W, dur=vdur(out, eng))

    def STT(out, in0, scalar, in1, op0, op1, R, W):
        return op("vector", lambda e: e.scalar_tensor_tensor(out=out, in0=in0, scalar=scalar, in1=in1, op0=op0, op1=op1), R, W,
                  dur=vdur(out))

    def TS(out, in0, s1, s2, op0, op1, R, W, eng="vector"):
        if s2 is None:
            return op(eng, lambda e: e.tensor_scalar(out=out, in0=in0, scalar1=s1, scalar2=None, op0=op0), R, W, dur=vdur(out, eng))
        return op(eng, lambda e: e.tensor_scalar(out=out, in0=in0, scalar1=s1, scalar2=s2, op0=op0, op1=op1), R, W, dur=vdur(out, eng))

    def CP(out, in_, R, W, eng="vector"):
        return op(eng, lambda e: e.tensor_copy(out=out, in_=in_), R, W, dur=vdur(out, eng))

    def bc(ap, shape):
        return ap.unsqueeze(2).broadcast_to(shape)

    cvec = P.sb([128, NCV], F32, "cvec")
    rowb = P.sb([128, NRB], F32, "rowb")
    ident = P.sb([128, 128], BF16, "ident")
    tri = P.sb([128, 3, 128], F32, "tri")
    bdf = P.sb([128, 16, 128], F32, "bdf")
    bd = P.sb([128, 16, 128], BF16, "bd")
    wdt = P.sb([128, 8, 16], BF16, "wdt")
    flag = P.sb([128, 1], F32, "flag")
    consts = [cvec, rowb, ident, tri, bdf, flag]
    P.dma("sync", flag[:], d_flag, "d_const", W=[flag])
    P.dma("sync", cvec[:], d_cvec, "d_const", W=[cvec])
    P.dma("sync", rowb[:], d_rowb, "d_const", W=[rowb])
    P.dma("sync", ident[:], d_ident, "d_const", W=[ident])
    P.dma("sync", tri[:], d_tri, "d_const", W=[tri])
    P.dma("sync", bdf[:], d_bd, "d_const", W=[bdf])
    for c in consts:
        c.buf.w = ("d_const", P.cnt["d_const"])
    P.dma("gpsimd", wdt[:], dram["w_in"].rearrange("(kc p) c -> p kc c", p=128)[:, :, 2560:2576], "d_wdt", W=[wdt])
    CP(bd[:], bdf[:], [bdf], [bd])
    U = tri[:, 0, :]
    G = tri[:, 1, :]
    ONES = tri[:, 2, :]
    cst = P.sb([128, 64], F32, "cst")
    ACT(cst[:, 0:16], rowb[:, R_ALOG:R_ALOG + 16], AF.Exp, [rowb], [cst])
    TS(cst[:, 0:16], cst[:, 0:16], -1.0, None, ALU.mult, None, [cst], [cst])
    ACT(cst[:, 32:40], cvec[:, O_LAM:O_LAM + 8], AF.Exp, [cvec], [cst], scale=-1.0)
    ACT(cst[:, 32:40], cst[:, 32:40], AF.Ln, [cst], [cst], bias=1.0)
    TS(cst[:, 16:24], cst[:, 32:40], -8.0, None, ALU.mult, None, [cst], [cst])
    TS(cst[:, 24:32], cst[:, 32:40], -16.0, None, ALU.mult, None, [cst], [cst])
    AROW = cst[:, 0:16]

    ssm_state = P.sb([128, 1024], F32, "ssm_state")
    ssm_state_bf = P.sb([128, 1024], BF16, "ssm_state_bf")
    lru_h = P.sb([128, 8], F32, "lru_h")
    halo_s = P.sb([128, 12, 3], F32, "halo_s")
    halo_l = P.sb([128, 8, 3], F32, "halo_l")
    halo_f = P.sb([128, 48, 2], F32, "halo_f")
    op("gpsimd", lambda e: e.memset(ssm_state[:], 0.0), [], [ssm_state])
    op("gpsimd", lambda e: e.memset(ssm_state_bf[:], 0.0), [], [ssm_state_bf])
    op("gpsimd", lambda e: e.memset(lru_h[:], 0.0), [], [lru_h])
    op("gpsimd", lambda e: e.memset(halo_s[:], 0.0), [], [halo_s])
    op("gpsimd", lambda e: e.memset(halo_l[:], 0.0), [], [halo_l])
    op("gpsimd", lambda e: e.memset(halo_f[:], 0.0), [], [halo_f])

    slots = [P.sb([128, 8, 512], BF16, f"slot{i}") for i in range(NSLOT)]
    ring = {"issued": 0, "res": 0, "done": 0}
    need = {}
    total_groups = len(GROUPS)

    wscr = nc.dram_tensor("wscr", [len(ALLG), 128, 8 * 512], BF16).ap()
    cvbuf = [Buf(f"cv{i}") for i in range(len(ALLG))]
    first_use = []
    for g in GROUPS:
        if g not in first_use:
            first_use.append(g)
    dgscr = nc.dram_tensor("dgscr", [20, 128, 512], BF16).ap()
    dgbuf = [Buf(f"dg{i}") for i in range(20)]
    conv_todo = list(first_use)

    def issue_conversions(k, only=None):
        for _ in range(min(k, len(conv_todo))):
            if only is not None:
                if only not in conv_todo:
                    return
                conv_todo.remove(only)
                gid = only
            else:
                gid = conv_todo.pop(0)
            parts = ALLG[gid]
            if isinstance(parts, tuple):
                parts = [parts + (512, 0)]
            for name, rb, c0, n, off in parts:
                src = dram[name][rb * 1024:(rb + 1) * 1024, :].rearrange("(kc p) c -> p kc c", p=128)[:, :, c0:c0 + n]
                dst = wscr[gid].rearrange("p (kc c) -> p kc c", c=512)[:, :, off:off + n]
                P.dma("gpsimd", dst, src, f"d_cv{gid}", W=[cvbuf[gid]], dur=12.0)

    n_first = len(_groups(modes[0]))
    issue_conversions(n_first)
    n_S = sum(1 for m in modes if m == "S")
    conv_per_tile = -(-(len(first_use) - n_first) // max(n_S, 1))

    def issue_load():
        gi = ring["issued"]
        gid = GROUPS[gi]
        issue_conversions(1, only=gid)
        sl = slots[gi % NSLOT]
        P.dma("sync", sl[:], wscr[gid].rearrange("p (kc c) -> p kc c", c=512), f"d_slot{gi % NSLOT}", R=[cvbuf[gid]], W=[sl], dur=5.0)
        ring["issued"] += 1

    def pump():
        while ring["done"] < ring["res"] and need.get(ring["done"], 1) == 0:
            ring["done"] += 1
        while ring["issued"] < total_groups and ring["issued"] < ring["done"] + NSLOT:
            issue_load()

    class GH:
        def __init__(self, n=1, readers=1):
            self.n, self.readers, self.first = n, readers, None

        def get(self):
            if self.first is None:
                self.first = ring["res"]
                ring["res"] += self.n
                for g in range(self.first, self.first + self.n):
                    need[g] = self.readers
            k = 0
            while True:
                pump()
                if ring["issued"] >= self.first + self.n:
                    break
                k += 1
                assert k < 200000, "ring deadlock"
                yield "blocked"
            return [slots[(self.first + i) % NSLOT] for i in range(self.n)]

        def release(self):
            def cb():
                for g in range(self.first, self.first + self.n):
                    need[g] -= 1
                pump()
            P.defer(cb)

    class RPool:
        def __init__(self, items):
            self.free = list(items)

        def get(self, n=1):
            k = 0
            while len(self.free) < n:
                k += 1
                assert k < 200000, "resource deadlock"
                yield "blocked"
            out = self.free[:n]
            del self.free[:n]
            return out

        def put(self, xs):
            xs = list(xs)
            P.defer(lambda: self.free.extend(xs))

    flags = {}

    def flag_inc(key):
        P.defer(lambda: flags.__setitem__(key, flags.get(key, 0) + 1))

    def flag_wait(key, n):
        while flags.get(key, 0) < n:
            yield "blocked"

    def seq(gen):
        try:
            while True:
                next(gen)
        except StopIteration as e:
            return e.value

    class Chain:
        def __init__(self, gen):
            self.gen, self.fifo, self.done = gen, [], False

    def advance(c):
        while not c.fifo and not c.done:
            P.fifo = c.fifo
            try:
                r = next(c.gen)
            except StopIteration:
                c.done = True
                r = None
            P.fifo = None
            if r == "blocked":
                return

    def run(chains, W, extra=(), extra_steps=1):
        act = [Chain(g) for g in extra]
        for c in act:
            c.extra = True
        it = iter(chains)
        pending = True
        nwin = 0
        guard = 0
        while True:
            while pending and sum(1 for c in act if not getattr(c, "extra", False)) < W:
                try:
                    act.append(Chain(next(it)))
                except StopIteration:
                    pending = False
            progressed = False
            for c in act:
                if not c.fifo and not c.done:
                    advance(c)
                while c.fifo and c.fifo[0][0] == "cb":
                    c.fifo.pop(0)[1]()
                    progressed = True
            before = len(act)
            act = [c for c in act if c.fifo or not c.done]
            if len(act) != before:
                progressed = True
            cands = [c for c in act if c.fifo]
            if not cands:
                if not act and not pending:
                    break
                guard += 1
                assert progressed or guard < 100000, "scheduler deadlock"
                continue
            guard = 0
            best = min(cands, key=lambda c: P.est_start(c.fifo[0]))
            P._emit(best.fifo.pop(0))

    ptrs = RPool([P.ps([128, 1024], BF16, f"ptr{i}") for i in range(2)])
    banks = RPool([P.ps([128, 512], F32, f"bank{i}") for i in range(6)])

    x_tok = [P.sb([128, 1024], F32, f"x_tok{j}") for j in range(NJ)]
    h_tok = [P.sb([128, 1024], BF16, "h_tok0")] * 2
    junk = P.sb([128, 512], BF16, "junk")
    hT = P.sb([128, 8, T], BF16, "hT")
    stat_n = P.sb([128, 8], F32, "stat_n")
    stat_g = [P.sb([128, 4], F32, f"stat_g{j}") for j in range(NJ)]
    stat_o = [P.sb([128, 4], F32, f"stat_o{j}") for j in range(NJ)]
    stat_f = P.sb([128, 16], F32, "stat_f")
    regA = P.sb([128, 24 * T], BF16, "regA")
    siluz = [Tile(regA[:, j * 1024:(j + 1) * 1024], f"siluz{j}") for j in range(NJ)]
    xsT = [Tile(regA[:, 4096 + c * T: 4096 + (c + 1) * T], f"xsT{c}") for c in range(8)]
    mergedT = Tile(regA[:, 4096:8192].rearrange("p (c t) -> p c t", t=T), "mergedT")
    xs_tok = [Tile(regA[:, 8192 + j * 1024: 8192 + (j + 1) * 1024], f"xs_tok{j}") for j in range(NJ)]
    fT = Tile(regA[:, :].rearrange("p (c t) -> p c t", t=T), "fT")
    regB = P.sb([128, 4 * 1024], F32, "regB")
    regB_bf = regB[:, :].bitcast(BF16)
    gy = [Tile(regB_bf[:, c * T:(c + 1) * T], f"gy{c}") for c in range(8)]
    y_bT = Tile(regB_bf[:, 4096:8192].rearrange("p (c t) -> p c t", t=T), "y_bT")
    d_tok = [Tile(regB[:, j * 1024:(j + 1) * 1024], f"d_tok{j}") for j in range(NJ)]
    BT = [P.sb([128, T], BF16, f"BT{g}") for g in range(2)]
    CT = [P.sb([128, T], BF16, f"CT{g}") for g in range(2)]
    B_tok = [P.sb([128, 256], BF16, f"B_tok{j}") for j in range(NJ)]
    y_aT = P.sb([128, 8, T], BF16, "y_aT")
    NGEN = 12
    gens = RPool([P.sb([128, 3 + T], F32, f"gen{i}") for i in range(NGEN)])
    bdf_flat = bdf[:, :, :].rearrange("p a b -> p (a b)")
    ssd_sets = []
    for i in range(2):
        st = {}
        st["dtb"] = P.sb([128, 64], F32, f"dtb{i}")
        st["exb"] = P.sb([128, 48], F32, f"exb{i}")
        st["ue4"] = P.sb([128, 4, 128], F32, f"ue4_{i}")
        st["exps"] = P.sb([128, 4, 128], F32, f"exps{i}")
        st["MT"] = P.sb([128, 16, 128], BF16, f"MT{i}")
        st["cbTm"] = P.sb([128, 2, 128], F32, f"cbTm{i}")
        st["ybuf"] = P.sb([128, 1024], F32, f"ybuf{i}")
        st["xdt"] = P.sb([128, 1024], BF16, f"xdt{i}")
        if i == 0:
            st["xs_dec"] = P.sb([128, 1024], BF16, "xs_dec0")
            st["ya_tok"] = P.sb([128, 1024], BF16, "ya_tok0")
            st["xsD"] = Tile(bdf_flat[:, 0:512].bitcast(BF16), "xsD0")
            st["alias"] = [st["xsD"]]
        else:
            st["xsD"] = Tile(bdf_flat[:, 512:1024].bitcast(BF16), "xsD1")
            st["xs_dec"] = Tile(bdf_flat[:, 1024:1536].bitcast(BF16), "xs_dec1")
            st["ya_tok"] = Tile(bdf_flat[:, 1536:2048].bitcast(BF16), "ya_tok1")
            st["alias"] = [st["xsD"], st["xs_dec"], st["ya_tok"]]
        ssd_sets.append(st)
    ssds = RPool(ssd_sets)
    state_ver = {}
    eps_t = P.sb([128, 1], F32, "eps_t")
    op("gpsimd", lambda e: e.memset(eps_t[:], EPS), [], [eps_t])

    def retire(olds, news):
        m = {}
        for o in olds:
            toks = list(o.buf.r)
            if o.buf.w is not None:
                toks.append(o.buf.w)
            for s, v in toks:
                m[s] = max(m.get(s, 0), v)
        for n in news:
            n.buf.w = None
            n.buf.r = list(m.items())

    def rstd_from_ss(out_ap, ss_ap, n, tl):
        ACT(out_ap, ss_ap, AF.Ln, [tl, eps_t], [tl], scale=1.0 / n, bias=eps_t[:, 0:1])
        ACT(out_ap, out_ap, AF.Exp, [tl], [tl], scale=-0.5)

    def norm_and_transpose(w_off, jl=range(NJ)):
        if len(jl) < NJ:
            op("gpsimd", lambda e: e.memset(stat_n[:, 0:4], 1.0), [], [stat_n])
        for j in jl:
            ACT(h_tok[0][:], x_tok[j][:], AF.Square, [x_tok[j]], [h_tok[0], stat_n], accum_out=stat_n[:, j:j + 1])
        rstd_from_ss(stat_n[:, 4:8], stat_n[:, 0:4], D, stat_n)
        for j in jl:
            ht = h_tok[j % 2]
            STT(ht[:], x_tok[j][:], stat_n[:, 4 + j:5 + j], rowb[:, w_off:w_off + D], ALU.mult, ALU.mult,
                [x_tok[j], stat_n, rowb], [ht])
            (ptr,) = seq(ptrs.get(1))
            for kc in range(8):
                op("tensor", lambda e, kc=kc, ht=ht, ptr=ptr: e.transpose(ptr[:, kc * 128:(kc + 1) * 128], ht[:, kc * 128:(kc + 1) * 128], ident[:]),
                   [ht, ident], [ptr], signal=(kc == 7), dur=0.12)
            CP(hT[:, :, j * 128:(j + 1) * 128], ptr[:, :].rearrange("p (c t) -> p c t", t=128), [ptr], [hT])
            ptrs.put([ptr])

    def fm_mm(b, sl, co, rhsT):
        P.mm(b[:], [(sl[:, kc, co:co + 128], rhsT[:, kc, :]) for kc in range(8)], R=[sl, rhsT], W=[b])

    def conv_steps(b, w, acc, halo, hidx, K, woff, boff, func, out_ap=None, out_tiles=None):
        H = K - 1
        CP(w[:, 0:H], halo[:, hidx, :], [halo], [w], eng="gpsimd")
        ACT(w[:, H:H + T], b[:], AF.Copy, [b], [w])
        ACT(acc[:, 0:T], b[:], AF.Identity, [b, cvec], [acc], scale=cvec[:, woff + K - 1:woff + K], bias=cvec[:, boff:boff + 1])
        banks.put([b])
        CP(halo[:, hidx, :], w[:, T:T + H], [w], [halo], eng="gpsimd")
        yield
        for k in range(K - 1):
            STT(acc[:, 0:T], w[:, k:k + T], cvec[:, woff + k:woff + k + 1], acc[:, 0:T], ALU.mult, ALU.add, [w, cvec, acc], [acc])
            yield
        if func is not None:
            if out_ap is None:
                ACT(acc[:, 0:T], acc[:, 0:T], func, [acc], [acc])
            else:
                ACT(out_ap, acc[:, 0:T], func, [acc], out_tiles)

    def view_bf(t, lo, hi, name):
        v = Tile(t[:, :].bitcast(BF16)[:, lo:hi], name)
        v.buf = t.buf
        return v

    def issue_dg(gd, ci, K=4):
        P.dma("scalar", gd[:, :].bitcast(BF16)[:, 0:K * 128], dgscr[ci], "d_dg_" + gd.buf.name, R=[dgbuf[ci]], W=[gd], dur=2.5)

    def conv_pe_steps(b, gw, gd, halo, hidx, K, ci):
        H = K - 1
        ubf = view_bf(gw, 0, H + T, "ubf")
        dg = [view_bf(gd, k * 128, (k + 1) * 128, f"dg{k}") for k in range(K)]
        CP(ubf[:, 0:H], halo[:, hidx, :], [halo], [gw], eng="gpsimd")
        ACT(ubf[:, H:H + T], b[:], AF.Copy, [b], [gw])
        banks.put([b])
        CP(halo[:, hidx, :], ubf[:, T:T + H], [gw], [halo], eng="gpsimd")
        yield
        (bc_,) = yield from banks.get(1)
        P.mm(bc_[:], [(dg[k][:], ubf[:, k:k + T]) for k in range(K)], R=[gd, gw], W=[bc_])
        yield
        return bc_

    def z_chain(gh, h, j, ti=0):
        (sl,) = yield from gh.get()
        (b,) = yield from banks.get(1)
        P.mm(b[:], [(hT[:, kc, j * 128:(j + 1) * 128], sl[:, kc, :]) for kc in range(8)], R=[sl, hT], W=[b])
        gh.release()
        yield
        ACT(siluz[j][:, h * 512:(h + 1) * 512], b[:], AF.Silu, [b], [siluz[j]])
        banks.put([b])
        flag_inc(("z", ti, j))

    def xbc_chain(gh, c, ch, mode="F", ti=0):
        (sl,) = yield from gh.get()
        if mode == "S" and ch >= 10:
            (b,) = yield from banks.get(1)
            P.mm(b[:, 0:3], [(sl[:, kc, c * 128:(c + 1) * 128], hT[:, kc, T - 3:T]) for kc in range(8)], R=[sl, hT], W=[b])
            gh.release()
            yield
            ACT(halo_s[:, ch, :], b[:, 0:3], AF.Copy, [b], [halo_s])
            banks.put([b])
            flag_inc(("xbc", ti))
            return
        w, acc = yield from gens.get(2)
        (b,) = yield from banks.get(1)
        fm_mm(b, sl, c * 128, hT)
        gh.release()
        yield
        if ch < 8:
            dst, dt_ = xsT[ch][:], [xsT[ch]]
        elif ch < 10:
            dst, dt_ = BT[ch - 8][:], [BT[ch - 8]]
        else:
            dst, dt_ = CT[ch - 10][:], [CT[ch - 10]]
        yield from conv_steps(b, w, acc, halo_s, ch, 4, O_CWS + ch * 4, O_CBS + ch, AF.Silu, dst, dt_)
        gens.put([w, acc])
        flag_inc(("xbc", ti))
        if ch < 10:
            flag_inc(("xsB", ti))

    def ly_chain(gh, c, ch, cs):
        n = cs.stop - cs.start
        (sl,) = yield from gh.get()
        (b,) = yield from banks.get(1)
        P.mm(b[:, 0:n], [(sl[:, kc, c * 128:(c + 1) * 128], hT[:, kc, cs]) for kc in range(8)], R=[sl, hT], W=[b])
        gh.release()
        yield
        ACT(gy[ch][:, cs], b[:, 0:n], AF.Gelu_apprx_tanh, [b], [gy[ch]])
        banks.put([b])

    def lx_chain(gh, c, ch, mode="F", cs=slice(0, T)):
        (sl,) = yield from gh.get()
        w, xcf, lr, li, la, lm = yield from gens.get(6)
        issue_dg(lm, 12 + ch)
        (b,) = yield from banks.get(1)
        fm_mm(b, sl, c * 128, hT)
        gh.release()
        yield
        bc_ = yield from conv_pe_steps(b, w, lm, halo_l, ch, 4, 12 + ch)
        xcb = view_bf(w, 516, 516 + T, "xcb")
        ACT(xcf[:, 0:T], bc_[:], AF.Identity, [bc_, cvec], [xcf], bias=cvec[:, O_CBL + ch:O_CBL + ch + 1])
        ACT(xcb[:], bc_[:], AF.Identity, [bc_, cvec], [w], bias=cvec[:, O_CBL + ch:O_CBL + ch + 1])
        banks.put([bc_])
        yield
        br_, bi_ = yield from banks.get(2)
        P.mm(br_[:], [(bd[:, ch, :], xcb[:])], R=[bd, xcb], W=[br_])
        P.mm(bi_[:], [(bd[:, 8 + ch, :], xcb[:])], R=[bd, xcb], W=[bi_])
        yield
        ACT(lr[:, 0:T], br_[:], AF.Sigmoid, [br_, cvec], [lr], bias=cvec[:, O_BR + ch:O_BR + ch + 1])
        ACT(li[:, 0:T], bi_[:], AF.Sigmoid, [bi_, cvec], [li], bias=cvec[:, O_BI + ch:O_BI + ch + 1])
        banks.put([br_, bi_])
        yield
        ACT(la[:, 0:T], lr[:, 0:T], AF.Exp, [lr, cst], [la], scale=cst[:, 16 + ch:17 + ch])
        ACT(lm[:, 0:T], lr[:, 0:T], AF.Exp, [lr, cst], [lm], scale=cst[:, 24 + ch:25 + ch])
        TT(li[:, 0:T], li[:, 0:T], xcf[:, 0:T], ALU.mult, [li, xcf], [li], eng="gpsimd")
        yield
        ACT(lm[:, 0:T], lm[:, 0:T], AF.Ln, [lm], [lm], scale=-1.0, bias=1.0)
        ACT(lm[:, 0:T], lm[:, 0:T], AF.Exp, [lm], [lm], scale=0.5)
        yield
        TT(li[:, 0:T], li[:, 0:T], lm[:, 0:T], ALU.mult, [li, lm], [li])
        yield
        op("vector", lambda e: e.tensor_tensor_scan(out=lr[:, 0:T], data0=la[:, 0:T], data1=li[:, 0:T], initial=lru_h[:, ch:ch + 1],
                                                    op0=ALU.mult, op1=ALU.add),
           [la, li, lru_h], [lr])
        yield
        CP(lru_h[:, ch:ch + 1], lr[:, T - 1:T], [lr], [lru_h], eng="gpsimd")
        if mode != "S":
            TT(y_bT[:, ch, cs], lr[:, cs], gy[ch][:, cs], ALU.mult, [lr, gy[ch]], [y_bT])
        gens.put([w, xcf, lr, li, la, lm])

    def ssd_chain(ti, j, mode="F", tmode="F"):
        if True:
            (st,) = yield from ssds.get(1)
            dtb, exb, MT, cbTm, xs_dec, ybuf, ya_tok, xsD, xdt = (st[k] for k in ("dtb", "exb", "MT", "cbTm", "xs_dec", "ybuf", "ya_tok", "xsD", "xdt"))
            ue4 = [st["ue4"]] * 2
            exps = [st["exps"]] * 2
            js = slice(j * 128, (j + 1) * 128)
            yield from flag_wait(("tr", ti, j), 1)
            if mode != "S":
                yield from flag_wait(("xbc", ti), 12)
                yield from flag_wait(("z", ti, j), 2)

            def bump():
                P.defer(lambda: state_ver.__setitem__(ti, state_ver.get(ti, 0) + 1))
            (bdt,) = yield from banks.get(1)
            P.mm(bdt[:, 0:16], [(hT[:, kc, js], wdt[:, kc, :]) for kc in range(8)], R=[hT, wdt], W=[bdt])
            TT(dtb[:, 0:16], bdt[:, 0:16], rowb[:, R_DTB:R_DTB + 16], ALU.add, [bdt, rowb], [dtb])
            banks.put([bdt])
            yield
            ACT(dtb[:, 0:16], dtb[:, 0:16], AF.Exp, [dtb], [dtb])
            ACT(dtb[:, 0:16], dtb[:, 0:16], AF.Ln, [dtb], [dtb], bias=1.0)
            TT(dtb[:, 16:32], dtb[:, 0:16], AROW, ALU.mult, [dtb, cst], [dtb])
            yield
            (bcs,) = yield from banks.get(1)
            P.mm(bcs[:, 0:16], [(U, dtb[:, 16:32])], R=[tri, dtb], W=[bcs], signal=False)
            P.mm(bcs[:, 16:32], [(G, dtb[:, 16:32])], R=[tri, dtb], W=[bcs], signal=False)
            P.mm(bcs[:, 32:48], [(ONES, dtb[:, 16:32])], R=[tri, dtb], W=[bcs])
            ACT(exb[:], bcs[:, 0:48], AF.Exp, [bcs], [exb])
            banks.put([bcs])
            yield
            TT(dtb[:, 32:48], dtb[:, 0:16], exb[:, 16:32], ALU.mult, [dtb, exb], [dtb])
            TT(xs_dec[:, :].rearrange("p (e d) -> p e d", d=64), xs_tok[j][:, :].rearrange("p (e d) -> p e d", d=64),
               bc(dtb[:, 32:48], [128, 16, 64]), ALU.mult, [xs_tok[j], dtb], [xs_dec])
            if mode != "S":
                TT(xsD[:, :].rearrange("p (e d) -> p e d", d=64), xs_tok[j][:, :].rearrange("p (e d) -> p e d", d=64),
                   bc(rowb[:, R_D:R_D + 16], [128, 16, 64]), ALU.mult, [xs_tok[j], rowb], [xsD], eng="gpsimd")
                TT(xdt[:, :].rearrange("p (e d) -> p e d", d=64), xs_tok[j][:, :].rearrange("p (e d) -> p e d", d=64),
                   bc(dtb[:, 0:16], [128, 16, 64]), ALU.mult, [xs_tok[j], dtb], [xdt], eng="gpsimd")
            yield
            if mode == "S":
                while state_ver.get(ti, 0) < j:
                    yield "blocked"
                bst = yield from banks.get(2)
                for g in range(2):
                    P.mm(bst[g][:], [(B_tok[j][:, g * 128:(g + 1) * 128], xs_dec[:, g * 512:(g + 1) * 512])],
                         R=[B_tok[j], xs_dec], W=[bst[g]])
                TT(ssm_state[:, :].rearrange("p (e d) -> p e d", d=64), ssm_state[:, :].rearrange("p (e d) -> p e d", d=64),
                   bc(exb[:, 32:48], [128, 16, 64]), ALU.mult, [ssm_state, exb], [ssm_state])
                for g in range(2):
                    gs = slice(g * 512, (g + 1) * 512)
                    TT(ssm_state[:, gs], ssm_state[:, gs], bst[g][:], ALU.add, [ssm_state, bst[g]], [ssm_state])
                banks.put(bst)
                if j == NJ - 1 or tmode == "H":
                    ACT(ssm_state_bf[:], ssm_state[:], AF.Copy, [ssm_state], [ssm_state_bf])
                bump()
                ssds.put([st])
                return
            (bcb,) = yield from banks.get(1)
            for g in range(2):
                P.mm(bcb[:, g * 128:(g + 1) * 128], [(BT[g][:, js], CT[g][:, js])], R=[BT[g], CT[g]], W=[bcb], signal=(g == 1))
            TT(cbTm[:, :, :], bcb[:, 0:256].rearrange("p (g l) -> p g l", l=128), U.unsqueeze(1).broadcast_to([128, 2, 128]),
               ALU.mult, [bcb, tri], [cbTm])
            banks.put([bcb])
            yield
            for q in range(4):
                u4 = ue4[q % 2]
                ex4 = exps[q % 2]
                TT(u4[:, :, :], U.unsqueeze(1).broadcast_to([128, 4, 128]), bc(dtb[:, 16 + q * 4:20 + q * 4], [128, 4, 128]),
                   ALU.mult, [tri, dtb], [u4], eng="gpsimd")
                yield
                (bsg,) = yield from banks.get(1)
                for e4 in range(4):
                    P.mm(bsg[:, e4 * 128:(e4 + 1) * 128], [(G, u4[:, e4, :])], R=[tri, u4], W=[bsg], signal=(e4 == 3))
                ACT(ex4[:, :, :], bsg[:, :].rearrange("p (e l) -> p e l", l=128), AF.Exp, [bsg], [ex4])
                banks.put([bsg])
                yield
                TT(MT[:, q * 4:(q + 1) * 4, :], ex4[:, :, :], cbTm[:, q // 2, :].unsqueeze(1).broadcast_to([128, 4, 128]), ALU.mult,
                   [ex4, cbTm], [MT])
                yield
            while state_ver.get(ti, 0) < j:
                yield "blocked"
            byo = yield from banks.get(2)
            for g in range(2):
                P.mm(byo[g][:], [(CT[g][:, js], ssm_state_bf[:, g * 512:(g + 1) * 512])], R=[CT[g], ssm_state_bf], W=[byo[g]])
            for g in range(2):
                gs = slice(g * 512, (g + 1) * 512)
                TT(ybuf[:, gs].rearrange("p (e d) -> p e d", d=64), byo[g][:, :].rearrange("p (e d) -> p e d", d=64),
                   bc(exb[:, g * 8:(g + 1) * 8], [128, 8, 64]), ALU.mult, [byo[g], exb], [ybuf])
            banks.put(byo)
            yield
            byd = yield from banks.get(2)
            for g in range(2):
                def fn(e, g=g, j=j, byd=byd):
                    e.matmul(byd[g][:], ident[:], xsD[:, g * 512:(g + 1) * 512], start=True, stop=False)
                    inst = None
                    for e8 in range(8):
                        e_ = g * 8 + e8
                        inst = e.matmul(byd[g][:, e8 * 64:(e8 + 1) * 64], MT[:, e_, :], xdt[:, e_ * 64:(e_ + 1) * 64],
                                        start=False, stop=(e8 == 7))
                    return inst
                op("tensor", fn, [MT, xdt, xsD, ident], [byd[g]], dur=0.3 + 8 * 0.1)
            for g in range(2):
                gs = slice(g * 512, (g + 1) * 512)
                TT(ybuf[:, gs], ybuf[:, gs], byd[g][:], ALU.add, [ybuf, byd[g]], [ybuf])
            banks.put(byd)
            yield
            bst = yield from banks.get(2)
            for g in range(2):
                P.mm(bst[g][:], [(B_tok[j][:, g * 128:(g + 1) * 128], xs_dec[:, g * 512:(g + 1) * 512])],
                     R=[B_tok[j], xs_dec], W=[bst[g]])
            TT(ssm_state[:, :].rearrange("p (e d) -> p e d", d=64), ssm_state[:, :].rearrange("p (e d) -> p e d", d=64),
               bc(exb[:, 32:48], [128, 16, 64]), ALU.mult, [ssm_state, exb], [ssm_state])
            for g in range(2):
                gs = slice(g * 512, (g + 1) * 512)
                TT(ssm_state[:, gs], ssm_state[:, gs], bst[g][:], ALU.add, [ssm_state, bst[g]], [ssm_state])
            banks.put(bst)
            ACT(ssm_state_bf[:], ssm_state[:], AF.Copy, [ssm_state], [ssm_state_bf])
            bump()
            yield
            if ti == 0:
                dump(f"y{j}", ybuf[:], [128, 1024], F32, [ybuf])
            sg = stat_g[j]
            TT(ybuf[:], ybuf[:], siluz[j][:], ALU.mult, [ybuf, siluz[j]], [ybuf])
            for g in range(2):
                ACT(junk[:, 0:512], ybuf[:, g * 512:(g + 1) * 512], AF.Square, [ybuf], [junk, sg], accum_out=sg[:, g:g + 1])
            rstd_from_ss(sg[:, 2:4], sg[:, 0:2], 512, sg)
            yield
            for g in range(2):
                gs = slice(g * 512, (g + 1) * 512)
                STT(ya_tok[:, gs], ybuf[:, gs], sg[:, 2 + g:3 + g], rowb[:, R_SN + g * 512:R_SN + (g + 1) * 512], ALU.mult, ALU.mult,
                    [ybuf, sg, rowb], [ya_tok])
            yield
            (ptr,) = yield from ptrs.get(1)
            for kc in range(8):
                op("tensor", lambda e, kc=kc, ptr=ptr: e.transpose(ptr[:, kc * 128:(kc + 1) * 128], ya_tok[:, kc * 128:(kc + 1) * 128], ident[:]),
                   [ya_tok, ident], [ptr], signal=(kc == 7), dur=0.12)
            CP(y_aT[:, :, js], ptr[:, :].rearrange("p (c t) -> p c t", t=128), [ptr], [y_aT])
            ptrs.put([ptr])
            ssds.put([st])

    def merge_chain(gh, c, m, cs):
        n = cs.stop - cs.start
        s_p, s_g = yield from gh.get()
        ga, gb, gt, gt2 = yield from gens.get(4)
        b3, b4 = yield from banks.get(2)

        def mmc(b, sl, co, rhsT):
            P.mm(b[:, 0:n], [(sl[:, kc, co:co + 128], rhsT[:, kc, cs]) for kc in range(8)], R=[sl, rhsT], W=[b])
        mmc(b3, s_g, c * 128, hT)
        mmc(b4, s_g, 256 + c * 128, hT)
        yield
        ACT(ga[:, 0:n], b3[:, 0:n], AF.Sigmoid, [b3, cvec], [ga], bias=cvec[:, O_GB + m:O_GB + m + 1])
        ACT(gb[:, 0:n], b4[:, 0:n], AF.Sigmoid, [b4, cvec], [gb], bias=cvec[:, O_GB + 8 + m:O_GB + 9 + m])
        banks.put([b3, b4])
        yield
        b1, b2 = yield from banks.get(2)
        mmc(b1, s_p, c * 128, y_aT)
        mmc(b2, s_p, 256 + c * 128, y_bT)
        gh.release()
        yield
        TT(gt[:, 0:n], ga[:, 0:n], b1[:, 0:n], ALU.mult, [ga, b1], [gt])
        TT(gt2[:, 0:n], gb[:, 0:n], b2[:, 0:n], ALU.mult, [gb, b2], [gt2])
        banks.put([b1, b2])
        yield
        TT(mergedT[:, m, cs], gt[:, 0:n], gt2[:, 0:n], ALU.add, [gt, gt2], [mergedT])
        gens.put([ga, gb, gt, gt2])

    def wout_chain(gh, j):
        js = slice(j * 128, (j + 1) * 128)
        so = stat_o[j]
        s_wo = yield from gh.get()
        yb, yb2 = yield from gens.get(2)
        bo = yield from banks.get(2)
        for h in range(2):
            P.mm(bo[h][:], [(mergedT[:, kc, js], s_wo[h][:, kc, :]) for kc in range(8)], R=[mergedT, s_wo[h]], W=[bo[h]])
        gh.release()
        yield
        for h in range(2):
            ACT(junk[:, 0:512], bo[h][:], AF.Square, [bo[h]], [junk, so], accum_out=so[:, h:h + 1])
        TT(so[:, 2:3], so[:, 0:1], so[:, 1:2], ALU.add, [so], [so])
        rstd_from_ss(so[:, 3:4], so[:, 2:3], D, so)
        yield
        ybs = [yb, yb2]
        for h in range(2):
            hs = slice(h * 512, (h + 1) * 512)
            STT(ybs[h][:, 0:T], bo[h][:], so[:, 3:4], rowb[:, R_PN1 + h * 512:R_PN1 + (h + 1) * 512], ALU.mult, ALU.mult,
                [bo[h], so, rowb], [ybs[h]])
        banks.put(bo)
        yield
        for h in range(2):
            hs = slice(h * 512, (h + 1) * 512)
            TT(x_tok[j][:, hs], x_tok[j][:, hs], ybs[h][:, 0:T], ALU.add, [x_tok[j], ybs[h]], [x_tok[j]])
        gens.put(ybs)

    def ffn_chain(gh, c, ch):
        s_g, s_v = yield from gh.get()
        wg, ag, wv, av = yield from gens.get(4)
        bg_, bv_ = yield from banks.get(2)
        fm_mm(bg_, s_g, c * 128, hT)
        fm_mm(bv_, s_v, c * 128, hT)
        gh.release()
        yield
        g1 = conv_steps(bg_, wg, ag, halo_f, ch, 3, O_CWF + ch * 3, O_CBF + ch, AF.Gelu_apprx_tanh)
        g2 = conv_steps(bv_, wv, av, halo_f, 24 + ch, 3, O_CWF + (24 + ch) * 3, O_CBF + 24 + ch, None)
        alive = [g1, g2]
        while alive:
            for g in list(alive):
                try:
                    next(g)
                except StopIteration:
                    alive.remove(g)
            yield
        TT(fT[:, ch, :], ag[:, 0:T], av[:, 0:T], ALU.mult, [ag, av], [fT], eng="gpsimd")
        gens.put([wg, ag, wv, av])

    def ffn_halo_chain(gh, c, ch):
        s_g, s_v = yield from gh.get()
        (b,) = yield from banks.get(1)
        P.mm(b[:, 0:2], [(s_g[:, kc, c * 128:(c + 1) * 128], hT[:, kc, T - 2:T]) for kc in range(8)], R=[s_g, hT], W=[b], signal=False)
        P.mm(b[:, 2:4], [(s_v[:, kc, c * 128:(c + 1) * 128], hT[:, kc, T - 2:T]) for kc in range(8)], R=[s_v, hT], W=[b])
        gh.release()
        yield
        ACT(halo_f[:, ch, :], b[:, 0:2], AF.Copy, [b], [halo_f])
        ACT(halo_f[:, 24 + ch, :], b[:, 2:4], AF.Copy, [b], [halo_f])
        banks.put([b])

    retire([bdf], ssd_sets[0]["alias"] + ssd_sets[1]["alias"])
    stg = seq(gens.get(2))
    for ci in range(20):
        g_ = stg[ci % 2]
        woff = (O_CWS + ci * 4) if ci < 12 else (O_CWL + (ci - 12) * 4)
        gv = g_[:, :].bitcast(BF16)
        for k in range(4):
            TS(gv[:, k * 128:(k + 1) * 128], ident[:], cvec[:, woff + k:woff + k + 1], None, ALU.mult, None, [ident, cvec], [g_])
        P.dma("sync", dgscr[ci], gv[:, 0:512], f"d_dgs{ci % 2}", R=[g_], W=[dgbuf[ci]])
    gens.put(stg)
    n_out = 0
    for ti in range(NT):
        t0 = ti * T
        mode = modes[ti]
        if mode == "F" and ti > 0 and modes[ti - 1] != "F":
            fl = flag[:, 0:1]
            TS(ssm_state[:], ssm_state[:], fl, None, ALU.mult, None, [ssm_state, flag], [ssm_state])
            ACT(ssm_state_bf[:], ssm_state[:], AF.Copy, [ssm_state], [ssm_state_bf])
            TS(lru_h[:], lru_h[:], fl, None, ALU.mult, None, [lru_h, flag], [lru_h])
            for hl in (halo_s, halo_l, halo_f):
                TS(hl[:], hl[:], fl, None, ALU.mult, None, [hl, flag], [hl])
        retire([fT], siluz + xsT + xs_tok)
        retire(d_tok, gy + [y_bT])
        for j in range(NJ):
            P.dma("sync", x_tok[j][:], dram["x"][t0 + j * 128:t0 + (j + 1) * 128, :], f"d_x{j}", W=[x_tok[j]])
        if mode != "S":
            issue_conversions(len(conv_todo))
        norm_and_transpose(R_W1)

        cs = slice(0, T) if mode == "F" else slice(T - 128, T)

        def tr_chain(j):
            yield from flag_wait(("xsB", ti), 10)
            (ptr,) = yield from ptrs.get(1)
            for c in range(8):
                op("tensor", lambda e, c=c, j=j, ptr=ptr: e.transpose(ptr[:, c * 128:(c + 1) * 128], xsT[c][:, j * 128:(j + 1) * 128], ident[:]),
                   [xsT[c], ident], [ptr], signal=(c == 7), dur=0.12)
            CP(xs_tok[j][:], ptr[:, :], [ptr], [xs_tok[j]])
            ptrs.put([ptr])
            yield
            (ptr,) = yield from ptrs.get(1)
            for g in range(2):
                op("tensor", lambda e, g=g, j=j, ptr=ptr: e.transpose(ptr[:, g * 128:(g + 1) * 128], BT[g][:, j * 128:(j + 1) * 128], ident[:]),
                   [BT[g], ident], [ptr], signal=(g == 1), dur=0.12)
            CP(B_tok[j][:], ptr[:, 0:256], [ptr], [B_tok[j]])
            ptrs.put([ptr])
            flag_inc(("tr", ti, j))

        def p23_chains():
            jz = range(NJ) if mode == "F" else [NJ - 1]
            for h in range(0 if mode == "S" else 2):
                gh = GH(1, len(jz))
                for j in jz:
                    yield z_chain(gh, h, j, ti)
            for h in range(3):
                gh = GH(1, 4)
                for c in range(4):
                    yield xbc_chain(gh, c, h * 4 + c, mode, ti)
            for h in range(0 if mode == "S" else 2):
                gh = GH(1, 4)
                for c in range(4):
                    yield ly_chain(gh, c, h * 4 + c, cs)
            for h in range(2):
                gh = GH(1, 4)
                for c in range(4):
                    yield lx_chain(gh, c, h * 4 + c, mode, cs)
        smode = lambda j: mode if (mode != "H" or j == NJ - 1) else "S"
        run(p23_chains(), 4, extra=[tr_chain(j) for j in range(NJ)] + [ssd_chain(ti, j, smode(j), mode) for j in range(NJ)])
        retire(xsT, [mergedT])
        if mode == "S":
            issue_conversions(conv_per_tile)
            continue
        if ti == 0:
            dump("y_aT", y_aT[:], [128, 8, T], BF16, [y_aT])
            dump("y_bT", y_bT[:], [128, 8, T], BF16, [y_bT])

        def p4_chains():
            for q in range(4):
                gh = GH(2, 2)
                for c in range(2):
                    yield merge_chain(gh, c, q * 2 + c, cs)
        run(p4_chains(), 3)
        jl_o = range(NJ) if mode == "F" else [NJ - 1]
        gh_wo = GH(2, len(jl_o))
        run((wout_chain(gh_wo, j) for j in jl_o), 2)
        if ti == 0:
            dump("mergedT", mergedT[:], [128, 8, T], BF16, [mergedT])
            for j in range(NJ):
                dump(f"x1_{j}", x_tok[j][:], [128, 1024], F32, [x_tok[j]])

        retire(siluz + [mergedT] + xs_tok, [fT])
        retire(gy + [y_bT], d_tok)
        if mode == "H":
            norm_and_transpose(R_W2, [NJ - 1])

            def p5h_chains():
                for q in range(6):
                    gh = GH(2, 4)
                    for c in range(4):
                        yield ffn_halo_chain(gh, c, q * 4 + c)
            run(p5h_chains(), 3)
            continue
        norm_and_transpose(R_W2)

        def p5_chains():
            for q in range(6):
                gh = GH(2, 4)
                for c in range(4):
                    yield ffn_chain(gh, c, q * 4 + c)
        run(p5_chains(), 3)
        if ti == 0:
            dump("fT", fT[:], [128, 24, T], BF16, [fT])
        for h in range(2):
            bd_ = seq(banks.get(NJ))
            for kb in range(3):
                ghd = GH(1, 1)
                (sl,) = seq(ghd.get())
                for j in range(NJ):
                    js = slice(j * 128, (j + 1) * 128)

                    def fn(e, j=j, kb=kb, sl=sl, js=js, bd_=bd_):
                        inst = None
                        for k8 in range(8):
                            inst = e.matmul(bd_[j][:], fT[:, kb * 8 + k8, js], sl[:, k8, :], start=(kb == 0 and k8 == 0),
                                            stop=(kb == 2 and k8 == 7))
                        return inst
                    op("tensor", fn, [fT, sl], [bd_[j]], signal=True)
                ghd.release()
            for j in range(NJ):
                hs = slice(h * 512, (h + 1) * 512)
                ACT(d_tok[j][:, hs], bd_[j][:], AF.Copy, [bd_[j]], [d_tok[j]])
                ACT(junk[:, 0:512], d_tok[j][:, hs], AF.Square, [d_tok[j]], [junk, stat_f], accum_out=stat_f[:, h * 4 + j:h * 4 + j + 1])
            banks.put(bd_)
        TT(stat_f[:, 8:12], stat_f[:, 0:4], stat_f[:, 4:8], ALU.add, [stat_f], [stat_f])
        rstd_from_ss(stat_f[:, 12:16], stat_f[:, 8:12], D, stat_f)
        for j in range(NJ):
            STT(d_tok[j][:], d_tok[j][:], stat_f[:, 12 + j:13 + j], rowb[:, R_PN2:R_PN2 + D], ALU.mult, ALU.mult,
                [d_tok[j], stat_f, rowb], [d_tok[j]])
            TT(x_tok[j][:], x_tok[j][:], d_tok[j][:], ALU.add, [x_tok[j], d_tok[j]], [x_tok[j]])
            P.dma("sync", d_out[n_out * T + j * 128:n_out * T + (j + 1) * 128, :], x_tok[j][:], f"d_o{j}", R=[x_tok[j]])
        n_out += 1

    P.wait_all("sync", [(f"d_o{j}", P.cnt[f"d_o{j}"]) for j in range(NJ)] + dbg_toks)
    P.emit()
    return nc


def prep_shared(inp):
    f = np.float32
    g = lambda k: np.asarray(inp[k], dtype=f)[0]
    sh = {}
    sh["w_in"] = np.ascontiguousarray(g("w_in"))
    sh["p_a"] = np.ascontiguousarray(g("w_proj_ssm"))
    sh["p_b"] = np.ascontiguousarray(g("w_proj_lru"))
    sh["w_out"] = np.ascontiguousarray(g("w_out"))
    sh["w_up"] = np.ascontiguousarray(g("w_ffn_up"))
    sh["w_down"] = np.ascontiguousarray(g("w_ffn_down"))
    bd = np.zeros((128, 16, 128), f)
    for gi, k in enumerate(("lru_wr", "lru_wi")):
        w = g(k)
        for ch in range(8):
            for hl in range(2):
                bd[hl * 64:(hl + 1) * 64, gi * 8 + ch, hl * 64:(hl + 1) * 64] = w[ch * 2 + hl]
    sh["bd"] = bd

    def cols(v):
        return np.ascontiguousarray(v.reshape(-1, 128).T)
    cv = np.zeros((128, NCV), f)
    cws = g("ssm_conv_w")
    for k in range(4):
        cv[:, O_CWS + k:O_CWS + 48:4] = cols(cws[k])
    cv[:, O_CBS:O_CBS + 12] = cols(g("ssm_conv_b"))
    cwl = g("lru_conv_w")
    for k in range(4):
        cv[:, O_CWL + k:O_CWL + 32:4] = cols(cwl[k])
    cv[:, O_CBL:O_CBL + 8] = cols(g("lru_conv_b"))
    cv[:, O_BR:O_BR + 8] = cols(g("lru_br").reshape(-1))
    cv[:, O_BI:O_BI + 8] = cols(g("lru_bi").reshape(-1))
    cv[:, O_LAM:O_LAM + 8] = cols(g("lru_lambda"))
    gbv = g("gate_b")
    cv[:, O_GB:O_GB + 8] = cols(gbv[0])
    cv[:, O_GB + 8:O_GB + 16] = cols(gbv[1])
    cwf = g("ffn_conv_w")
    for k in range(3):
        cv[:, O_CWF + k:O_CWF + 144:3] = cols(cwf[k])
    cv[:, O_CBF:O_CBF + 48] = cols(g("ffn_conv_b"))
    sh["cvec"] = cv
    rb = np.zeros((128, NRB), f)
    rb[:, R_PN1:R_PN1 + D] = g("mix_post_norm")[None]
    rb[:, R_PN2:R_PN2 + D] = g("ffn_post_norm")[None]
    rb[:, R_SN:R_SN + D] = g("ssm_norm")[None]
    rb[:, R_W1:R_W1 + D] = g("mix_pre_norm")[None]
    rb[:, R_W2:R_W2 + D] = g("ffn_pre_norm")[None]
    rb[:, R_DTB:R_DTB + 16] = g("ssm_dt_bias")[None]
    rb[:, R_ALOG:R_ALOG + 16] = g("ssm_a_log")[None]
    rb[:, R_D:R_D + 16] = g("ssm_d")[None]
    sh["rowb"] = rb
    sh["ident"] = np.eye(128).astype(ml_dtypes.bfloat16)
    k = np.arange(128)
    tri = np.zeros((128, 3, 128), f)
    tri[:, 0, :] = (k[:, None] <= k[None, :])
    tri[:, 1, :] = (k[:, None] > k[None, :])
    tri[:, 2, :] = 1.0
    sh["tri"] = tri
    return sh


PREFIX_MODES = ["S", "S", "S", "H"]


def kernel(**inputs):
    x = np.asarray(inputs["x"], dtype=np.float32)
    B, S, _ = x.shape
    sh = prep_shared(inputs)
    half = S // 2
    n_own = half // T
    modes = PREFIX_MODES[-(half // T):] + ["F"] * n_own
    nc = build_program(modes)
    in_maps = []
    for c in range(8):
        b, hf = c // 2, c % 2
        m = dict(sh)
        pre = x[b, :half] if hf else np.zeros((half, D), np.float32)
        m["x"] = np.ascontiguousarray(np.concatenate([pre, x[b, hf * half:(hf + 1) * half]], axis=0))
        m["flag"] = np.full((128, 1), float(hf), np.float32)
        in_maps.append(m)
    res = run_bass_kernel_spmd(nc, in_maps, core_ids=list(range(8)))
    out = np.empty((B, S, D), np.float32)
    for c in range(8):
        b, hf = c // 2, c % 2
        out[b, hf * half:(hf + 1) * half] = res.results[c]["out"]
    return out
```
